# Optimizing a Trainium2 kernel written in Bass

```python
import math
import jax, jax.numpy as jnp
from jax import lax
import numpy as np

D_MODEL = 2048
BATCH = 16
SEQ = 2048
DEPTH = 1

HEAD_DIM = 128
D_MIX = D_MODEL
GLA_HEADS = D_MIX // 2 // HEAD_DIM
GLA_DK = HEAD_DIM // 2
GLA_DV = HEAD_DIM
GLA_RANK = 16
GLA_TAU = 16.0
GLA_CHUNK = 64
FOX_HEADS = D_MIX // 4 // HEAD_DIM
FOX_BLOCK = 128
MEM_HEADS = 4
MEM_TOKENS = 256
N_OUT_HEADS = GLA_HEADS + FOX_HEADS + MEM_HEADS
D_FF = 4 * D_MODEL
EPS = 1e-6

kernel_name = "hymba_gla_fox_memory_layer"


def _in_widths():
    return (
        GLA_HEADS * GLA_DK,
        GLA_HEADS * GLA_DK,
        GLA_HEADS * GLA_DV,
        GLA_HEADS * GLA_DV,
        GLA_RANK,
        FOX_HEADS * HEAD_DIM,
        FOX_HEADS * HEAD_DIM,
        FOX_HEADS * HEAD_DIM,
        FOX_HEADS * HEAD_DIM,
        FOX_HEADS,
        MEM_HEADS * HEAD_DIM,
        MEM_HEADS * HEAD_DIM,
    )


def _split_points():
    pts, acc = [], 0
    for w in _in_widths()[:-1]:
        acc += w
        pts.append(acc)
    return pts


def rms_norm(x, g):
    xf = x.astype(jnp.float32)
    y = xf * lax.rsqrt(jnp.mean(xf * xf, axis=-1, keepdims=True) + EPS)
    return (y * g.astype(jnp.float32)).astype(x.dtype)


def _heads(t, h):
    return t.reshape(t.shape[0], t.shape[1], h, -1).transpose(0, 2, 1, 3)


def gla_chunked(q, k, v, log_a):
    B, H, S, DK = q.shape
    DV = v.shape[-1]
    C = GLA_CHUNK
    n = S // C

    def to_chunks(t):
        return t.reshape(B, H, n, C, t.shape[-1]).transpose(2, 0, 1, 3, 4)

    qc = to_chunks(q * (DK ** -0.5))
    kc, vc, gc = to_chunks(k), to_chunks(v), to_chunks(log_a)
    causal = jnp.tril(jnp.ones((C, C), dtype=bool))[:, :, None]

    def step(state, xs):
        qi, ki, vi, gi = xs
        b = jnp.cumsum(gi.astype(jnp.float32), axis=2)
        inter = jnp.einsum('bhck,bhkv->bhcv', qi * jnp.exp(b), state)
        rel = b[:, :, :, None, :] - b[:, :, None, :, :]
        decay = jnp.exp(jnp.where(causal, rel, -jnp.inf))
        scores = jnp.einsum('bhik,bhijk,bhjk->bhij', qi.astype(jnp.float32), decay,
                            ki.astype(jnp.float32))
        intra = jnp.einsum('bhij,bhjv->bhiv', scores, vi.astype(jnp.float32))
        b_last = b[:, :, -1:, :]
        new_state = (jnp.exp(b_last[:, :, 0, :])[..., None] * state
                     + jnp.einsum('bhck,bhcv->bhkv', ki * jnp.exp(b_last - b),
                                  vi.astype(jnp.float32)))
        return new_state, inter + intra

    state0 = jnp.zeros((B, H, DK, DV), jnp.float32)
    _, out = lax.scan(step, state0, (qc, kc, vc, gc))
    return out.transpose(1, 2, 0, 3, 4).reshape(B, H, S, DV)


def forgetting_attention(q, k, v, log_f):
    B, H, S, D = q.shape
    scale = D ** -0.5
    c = jnp.cumsum(log_f.astype(jnp.float32), axis=-1)
    outs = []
    for i in range(S // FOX_BLOCK):
        q0, q1 = i * FOX_BLOCK, (i + 1) * FOX_BLOCK
        qb, kb, vb = q[:, :, q0:q1], k[:, :, :q1], v[:, :, :q1]
        logits = (jnp.einsum('bhqd,bhkd->bhqk', qb, kb).astype(jnp.float32) * scale
                  + c[:, :, q0:q1, None] - c[:, :, None, :q1])
        mask = (q0 + jnp.arange(FOX_BLOCK))[:, None] >= jnp.arange(q1)[None, :]
        p = jax.nn.softmax(jnp.where(mask, logits, -jnp.inf), axis=-1)
        outs.append(jnp.einsum('bhqk,bhkd->bhqd', p.astype(v.dtype), vb))
    return jnp.concatenate(outs, axis=2)


def setup_inputs(seed: int = 0) -> dict:
    key = jax.random.key(seed)
    ks = jax.random.split(key, 20)
    n = jax.random.normal
    d_in = sum(_in_widths())
    return {
        "x": n(ks[0], (BATCH, SEQ, D_MODEL), jnp.float32),
        "mem": n(ks[1], (BATCH, MEM_TOKENS, D_MODEL), jnp.float32),
        "attn_norm_g": 1.0 + 0.02 * n(ks[2], (D_MODEL,), jnp.float32),
        "w_in": n(ks[3], (D_MODEL, d_in), jnp.float32) * D_MODEL ** -0.5,
        "gla_a_w2": n(ks[4], (GLA_RANK, GLA_HEADS * GLA_DK), jnp.float32) * GLA_RANK ** -0.5,
        "gla_a_b": 0.5 * n(ks[5], (GLA_HEADS * GLA_DK,), jnp.float32),
        "fox_f_b": 3.0 + 0.5 * n(ks[6], (FOX_HEADS,), jnp.float32),
        "fox_q_norm_g": 1.0 + 0.02 * n(ks[7], (HEAD_DIM,), jnp.float32),
        "fox_k_norm_g": 1.0 + 0.02 * n(ks[8], (HEAD_DIM,), jnp.float32),
        "mem_norm_g": 1.0 + 0.02 * n(ks[9], (D_MODEL,), jnp.float32),
        "w_mem_kv": n(ks[10], (D_MODEL, 2 * MEM_HEADS * HEAD_DIM), jnp.float32) * D_MODEL ** -0.5,
        "mem_q_norm_g": 1.0 + 0.02 * n(ks[11], (HEAD_DIM,), jnp.float32),
        "mem_k_norm_g": 1.0 + 0.02 * n(ks[12], (HEAD_DIM,), jnp.float32),
        "out_norm_g": 1.0 + 0.02 * n(ks[13], (N_OUT_HEADS * HEAD_DIM,), jnp.float32),
        "w_out": n(ks[14], (N_OUT_HEADS * HEAD_DIM, D_MODEL), jnp.float32) * (N_OUT_HEADS * HEAD_DIM) ** -0.5,
        "mlp_norm_g": 1.0 + 0.02 * n(ks[15], (D_MODEL,), jnp.float32),
        "w_up": n(ks[16], (D_MODEL, D_FF), jnp.float32) * D_MODEL ** -0.5,
        "w_down": n(ks[17], (D_FF, D_MODEL), jnp.float32) * D_FF ** -0.5,
    }


def reference(x, mem, attn_norm_g, w_in, gla_a_w2, gla_a_b, fox_f_b, fox_q_norm_g,
              fox_k_norm_g, mem_norm_g, w_mem_kv, mem_q_norm_g, mem_k_norm_g,
              out_norm_g, w_out, mlp_norm_g, w_up, w_down):
    B, S, _ = x.shape
    h = x
    for _layer in range(DEPTH):
        xn = rms_norm(h, attn_norm_g)
        proj = xn @ w_in
        (gq, gk, gv, gg, ga, fq, fk, fv, fg, ff, mq, mg) = jnp.split(proj, _split_points(), axis=-1)

        log_a = jax.nn.log_sigmoid((ga @ gla_a_w2 + gla_a_b).astype(jnp.float32)) / GLA_TAU
        gla_o = gla_chunked(_heads(gq, GLA_HEADS), _heads(gk, GLA_HEADS),
                            _heads(gv, GLA_HEADS), _heads(log_a, GLA_HEADS))
        gla_o = gla_o.astype(x.dtype).transpose(0, 2, 1, 3)

        fq_h = rms_norm(_heads(fq, FOX_HEADS), fox_q_norm_g)
        fk_h = rms_norm(_heads(fk, FOX_HEADS), fox_k_norm_g)
        log_f = jax.nn.log_sigmoid((ff + fox_f_b).astype(jnp.float32)).transpose(0, 2, 1)
        fox_o = forgetting_attention(fq_h, fk_h, _heads(fv, FOX_HEADS), log_f)
        fox_o = fox_o.astype(x.dtype).transpose(0, 2, 1, 3)

        mn = rms_norm(mem, mem_norm_g)
        mk, mv = jnp.split(mn @ w_mem_kv, 2, axis=-1)
        mk = rms_norm(mk.reshape(B, mem.shape[1], MEM_HEADS, HEAD_DIM), mem_k_norm_g)
        mv = mv.reshape(B, mem.shape[1], MEM_HEADS, HEAD_DIM)
        mq_h = rms_norm(mq.reshape(B, S, MEM_HEADS, HEAD_DIM), mem_q_norm_g)
        m_logits = jnp.einsum('bshd,bmhd->bhsm', mq_h, mk).astype(jnp.float32) * HEAD_DIM ** -0.5
        m_p = jax.nn.softmax(m_logits, axis=-1)
        mem_o = jnp.einsum('bhsm,bmhd->bshd', m_p.astype(mv.dtype), mv)

        o = jnp.concatenate([gla_o, fox_o, mem_o], axis=2)
        o = rms_norm(o, out_norm_g.reshape(N_OUT_HEADS, HEAD_DIM)).reshape(B, S, -1)
        gate = jnp.concatenate([jax.nn.silu(gg), jax.nn.sigmoid(fg), jax.nn.sigmoid(mg)], axis=-1)
        h = h + (o * gate) @ w_out

        u = jax.nn.relu(rms_norm(h, mlp_norm_g) @ w_up)
        h = h + (u * u) @ w_down
    return h
```

```python
import numpy as np
from contextlib import ExitStack
import concourse.bass as bass
import concourse.mybir as mybir
from concourse.bass_utils import run_bass_kernel_spmd

F32 = mybir.dt.float32
BF16 = mybir.dt.bfloat16
AF = mybir.ActivationFunctionType
ALU = mybir.AluOpType
P = 128
D = 2048
KC = 16
EPS = 1e-6
NRING = 3


class Tracker:
    def __init__(self, nc, st):
        self.nc = nc
        self.st = st
        self.eng = {'pe': nc.tensor, 'act': nc.scalar, 'dve': nc.vector, 'pool': nc.gpsimd, 'sp': nc.sync}
        self.sem = {e: st.enter_context(nc.semaphore('s_' + e)) for e in ('pe', 'act', 'dve')}
        self.cnt = {e: 0 for e in self.sem}
        self.seen = {e: {} for e in self.eng}
        self.lw = {}
        self.rd = {}
        self.dsem = {}
        self.dcnt = {}
        self.out_toks = {}

    def _wait(self, eng, toks):
        need = {}
        seen = self.seen[eng]
        for (key, sem, val) in toks:
            if seen.get(key, 0) >= val:
                continue
            if key not in need or need[key][1] < val:
                need[key] = (sem, val)
        for key, (sem, val) in need.items():
            self.eng[eng].wait_ge(sem, val)
            seen[key] = val

    def _collect(self, eng, reads, writes):
        toks = []
        for r in reads:
            t = self.lw.get(r)
            if t is not None and not (t[3] == eng and eng == 'pe'):
                toks.append(t[:3])
        for w in writes:
            t = self.lw.get(w)
            if t is not None and (t[3] != eng or eng is None):
                toks.append(t[:3])
            for t in self.rd.get(w, ()):
                if t[3] != eng or eng is None:
                    toks.append(t[:3])
        return toks

    def _commit(self, tok, reads, writes):
        for w in writes:
            self.lw[w] = tok
            self.rd[w] = []
        for r in reads:
            self.rd.setdefault(r, []).append(tok)

    def op(self, eng, fn, reads=(), writes=()):
        psr = [r for r in reads if r[0] == 'ps']
        if psr:
            reads = [r for r in reads if r[0] != 'ps']
            writes = list(writes) + [r for r in psr if r not in writes]
        self._wait(eng, self._collect(eng, reads, writes))
        ins = fn(self.eng[eng])
        self.cnt[eng] += 1
        ins.then_inc(self.sem[eng], 1)
        self._commit((eng, self.sem[eng], self.cnt[eng], eng), reads, writes)

    def dma(self, q, key, fn, reads=(), writes=(), is_out=False):
        self._wait(q, self._collect(None, reads, writes))
        if key not in self.dsem:
            self.dsem[key] = self.st.enter_context(self.nc.semaphore('d_' + '_'.join(str(k) for k in key)))
            self.dcnt[key] = 0
        ins = fn(self.eng[q])
        self.dcnt[key] += 16
        ins.then_inc(self.dsem[key], 16)
        tok = (('d',) + tuple(key), self.dsem[key], self.dcnt[key], None)
        self._commit(tok, reads, writes)
        if is_out:
            self.out_toks[key] = tok

    def barrier(self):
        toks = [(e, self.sem[e], self.cnt[e]) for e in self.sem if self.cnt[e]]
        toks += [(('d',) + tuple(k), self.dsem[k], self.dcnt[k]) for k in self.dsem if k[0] != 'ring']
        for e in ('pe', 'act', 'dve', 'sp'):
            self._wait(e, toks)

    def finish(self):
        self._wait('sp', [t[:3] for t in self.out_toks.values()])


def win_cols(name, i):
    base = {'gq': 0, 'gk': 512, 'gv': 1024, 'gg': 2048, 'fq': 3088, 'fk': 3600, 'fv': 4112,
            'fg': 4624, 'mq': 5140, 'mg': 5652}[name]
    return list(range(base + 128 * i, base + 128 * i + 128))


def win_block_list():
    blocks = [('small', 0)]
    for nm, n in (('gq', 4), ('gk', 4), ('gv', 8), ('gg', 8), ('fq', 4), ('fk', 4), ('fv', 4), ('fg', 4),
                  ('mq', 4), ('mg', 4)):
        blocks += [(nm, i) for i in range(n)]
    return blocks


WIN_BLOCKS = win_block_list()
WIN_IDX = {b: i for i, b in enumerate(WIN_BLOCKS)}


def build(NSEQ, S, stop=None):
    NT = S // P
    CW = min(512, S)
    NCH = S // CW
    TPC = CW // P
    NM = 256
    nc = bass.Bass("TRN2", target_bir_lowering=False)

    def din(name, shape):
        return nc.dram_tensor(name, list(shape), F32, kind="ExternalInput").ap()

    x = din("x", [NSEQ * S, D])
    mem = din("mem", [NSEQ * NM, D])
    consts = din("consts", [P, 384])
    pvec = din("pvec", [P, 20])
    gvecs = din("gvecs", [3, D])
    w2d = din("w2", [16, 512])
    rowp = din("rowp", [1, 516])
    win = din("win", [49, P, 2048])
    wmem = din("wmem", [8, P, 2048])
    wout = din("wout", [16, P, 2048])
    wup = din("wup", [64, P, 2048])
    wdn = din("wdn", [64, P, 2048])
    out = nc.dram_tensor("out", [NSEQ * S, D], F32, kind="ExternalOutput").ap()

    with ExitStack() as st:
        T = Tracker(nc, st)

        uid = [0]

        def sb(name, shape, dt, stack=st):
            uid[0] += 1
            return stack.enter_context(nc.sbuf_tensor(f"{name}_{uid[0]}", list(shape), dt))

        ps = [st.enter_context(nc.psum_tensor(f"ps{i}", [P, 512], F32)) for i in range(8)]
        TB = (7, 6)

        def PK(b, lo=0, n=4):
            return [('ps', b)]

        def ACT(fn, r=(), w=()):
            T.op('act', fn, r, w)

        def DVE(fn, r=(), w=()):
            T.op('dve', fn, r, w)

        def PE(fn, r=(), w=()):
            T.op('pe', fn, r, w)

        big1 = sb("big1", [P, KC, S], BF16)
        big2 = sb("big2", [P, max(16 * S, 64 * CW)], BF16)
        ogT = big2[:, 0:16 * S].rearrange("p (h s) -> p h s", h=16)
        uT = big2[:, 0:64 * CW].rearrange("p (f t) -> p f t", f=64)
        ring = sb("ring", [P, NRING, 2048], BF16)
        cf = sb("cf", [P, 384], F32)
        identb = sb("identb", [P, P], BF16)
        Ub = sb("Ub", [P, P], BF16)
        onesb = sb("onesb", [P, P], BF16)
        onesDb = sb("onesDb", [P, P], BF16)
        U16 = sb("U16", [P, P], F32)
        pv = sb("pv", [P, 20], F32)
        w2f = sb("w2f", [16, 512], F32)
        w2b = sb("w2b", [16, 512], BF16)
        rowps = sb("rowps", [1, 516], F32)
        epsc = sb("epsc", [P, 1], F32)
        mkT = sb("mkT", [P, 4, NM], BF16)
        mv = sb("mv", [P, 2, 512], BF16)
        ssb = sb("ssb", [P, 8], F32)
        rsb = sb("rsb", [P, 8], F32)
        identf = cf[:, 0:128]
        Uf = cf[:, 128:256]
        onesf = cf[:, 256:384]

        T.dma('sp', ('c', 0), lambda e: e.dma_start(out=cf[:], in_=consts), writes=[('cf',)])
        T.dma('sp', ('c', 1), lambda e: e.dma_start(out=pv[:], in_=pvec), writes=[('pv',)])
        T.dma('sp', ('c', 2), lambda e: e.dma_start(out=w2f[:], in_=w2d), writes=[('w2f',)])
        T.dma('sp', ('c', 3), lambda e: e.dma_start(out=rowps[:], in_=rowp), writes=[('rowps',)])
        DVE(lambda e: e.tensor_copy(out=identb[:], in_=identf), [('cf',)], [('k1',)])
        DVE(lambda e: e.tensor_copy(out=Ub[:], in_=Uf), [('cf',)], [('k2',)])
        DVE(lambda e: e.tensor_copy(out=onesb[:], in_=onesf), [('cf',)], [('k3',)])
        DVE(lambda e: e.tensor_scalar(out=onesDb[:], in0=onesf, scalar1=1.0 / 128, scalar2=None, op0=ALU.mult),
            [('cf',)], [('k4',)])
        DVE(lambda e: e.tensor_scalar(out=U16[:], in0=Uf, scalar1=1.0 / 16, scalar2=None, op0=ALU.mult),
            [('cf',)], [('k5',)])
        DVE(lambda e: e.tensor_copy(out=w2b[:], in_=w2f[:]), [('w2f',)], [('k6',)])
        DVE(lambda e: e.memset(epsc[:], EPS), [], [('k7',)])
        T.barrier()
        if stop == 'INIT':
            T.finish()
            return nc

        wcount = [0]

        def wload(src):
            slot = wcount[0] % NRING
            wcount[0] += 1
            T.dma('pool', ('ring', slot), lambda e: e.dma_start(out=ring[:, slot, :], in_=src),
                  writes=[('ring', slot)])
            return slot

        rot = [0]

        def gbank(banks=(0, 1)):
            b = banks[rot[0] % len(banks)]
            rot[0] += 1
            return b

        def proj_fm(slot, srcT, ksrc, ch, bank, n=None, c0=None):
            n = CW if n is None else n
            c0 = ch * CW if c0 is None else c0

            def f(e):
                for kc in range(KC):
                    ins = e.matmul(ps[bank][:, 0:n], lhsT=ring[:, slot, kc * 128:(kc + 1) * 128],
                                   rhs=srcT[:, kc, c0:c0 + n], start=(kc == 0), stop=(kc == KC - 1))
                return ins
            PE(f, [('ring', slot)] + ksrc, PK(bank))

        def rms_cols(src, ksrc, n, tmp, par):
            sq, rs = tmp['sq'][par], tmp['rs'][par]
            ACT(lambda e: e.activation(out=sq[:, 0:n], in_=src, func=AF.Square), ksrc, [('sq', par)])
            PE(lambda e: e.matmul(ps[6][:, 0:n], lhsT=onesDb[:], rhs=sq[:, 0:n], start=True, stop=True),
               [('sq', par)], PK(6))
            ACT(lambda e: e.activation(out=rs[:, 0:n], in_=ps[6][:, 0:n], func=AF.Sqrt, bias=epsc[:], scale=1.0),
                PK(6), [('rs', par)])
            DVE(lambda e: e.reciprocal(out=rs[:, 0:n], in_=rs[:, 0:n]), [('rs', par)], [('rs', par)])
            return rs

        def qknorm(bank, gain, dst, kdst, n, tmp, par):
            rs = rms_cols(ps[bank][:, 0:n], PK(bank), n, tmp, par)
            DVE(lambda e: e.scalar_tensor_tensor(out=dst, in0=ps[bank][:, 0:n], scalar=gain, in1=rs[:, 0:n],
                                                 op0=ALU.mult, op1=ALU.mult),
                PK(bank) + [('rs', par)], kdst)

        def epilogue(obank, dbank, gate, kgate, head, t0, n, tmp, par):
            if dbank is not None:
                rden, on = tmp['rden'][par], tmp['on'][par]
                DVE(lambda e: e.reciprocal(out=rden[:, 0:n], in_=ps[dbank][:, 0:n]), PK(dbank), [('rden', par)])
                DVE(lambda e: e.tensor_tensor(out=on[:, 0:n], in0=ps[obank][:, 0:n], in1=rden[:, 0:n], op=ALU.mult),
                    PK(obank) + [('rden', par)], [('on', par)])
                src, ksrc = on[:, 0:n], [('on', par)]
            else:
                src, ksrc = ps[obank][:, 0:n], PK(obank)
            rs = rms_cols(src, ksrc, n, tmp, par)
            DVE(lambda e: e.tensor_tensor(out=rs[:, 0:n], in0=rs[:, 0:n], in1=gate, op=ALU.mult),
                [('rs', par)] + kgate, [('rs', par)])
            DVE(lambda e: e.scalar_tensor_tensor(out=ogT[:, head, t0:t0 + n], in0=src,
                                                 scalar=pv[:, 4 + head:5 + head], in1=rs[:, 0:n],
                                                 op0=ALU.mult, op1=ALU.mult),
                ksrc + [('rs', par)], [('og', head, t0 // CW)])

        def logsig(zsrc, kz, dst, kdst, n, tmp, ka=('lsa',), kl=('lsl',)):
            a, l = tmp['lsa'], tmp['lsl']
            ACT(lambda e: e.activation(out=a[:, 0:n], in_=zsrc, func=AF.Abs), kz, [ka])
            ACT(lambda e: e.activation(out=a[:, 0:n], in_=a[:, 0:n], func=AF.Exp, scale=-1.0), [ka], [ka])
            ACT(lambda e: e.activation(out=l[:, 0:n], in_=a[:, 0:n], func=AF.Ln, bias=1.0, scale=1.0),
                [ka], [kl])
            DVE(lambda e: e.scalar_tensor_tensor(out=dst, in0=zsrc, scalar=0.0, in1=l[:, 0:n],
                                                 op0=ALU.min, op1=ALU.subtract),
                kz + [kl], kdst)

        def norm_rows(src, ksrc, gbc, xnb, par, dstT, tok0, cell):
            DVE(lambda e: e.memset(ssb[:, cell:cell + 1], 0.0), [], [('ss', cell)])
            ACT(lambda e: e.activation(out=sqj[:], in_=src, func=AF.Square, accum_out=ssb[:, cell:cell + 1]),
                ksrc + [('ss', cell)], [('sqj',), ('ss', cell)])
            ACT(lambda e: e.activation(out=rsb[:, cell:cell + 1], in_=ssb[:, cell:cell + 1], func=AF.Sqrt,
                                       bias=epsc[:], scale=1.0 / D), [('ss', cell)], [('rsb', cell)])
            DVE(lambda e: e.reciprocal(out=rsb[:, cell:cell + 1], in_=rsb[:, cell:cell + 1]),
                [('rsb', cell)], [('rsb', cell)])
            DVE(lambda e: e.scalar_tensor_tensor(out=xnb[:, par, :], in0=src, scalar=rsb[:, cell:cell + 1],
                                                 in1=gbc[:], op0=ALU.mult, op1=ALU.mult),
                ksrc + [('rsb', cell), ('gbc',)], [('xnb', par)])
            for q in range(4):
                h = q % 2

                def f(e, q=q, h=h):
                    for i in range(4):
                        kc = q * 4 + i
                        ins = e.matmul(ps[TB[h]][:, i * 128:(i + 1) * 128], lhsT=xnb[:, par, kc * 128:(kc + 1) * 128],
                                       rhs=identb[:], start=True, stop=True)
                    return ins
                PE(f, [('xnb', par)], PK(TB[h]))
                ACT(lambda e, q=q, h=h: e.activation(
                    out=dstT[:, q * 4:(q + 1) * 4, tok0:tok0 + P],
                    in_=ps[TB[h]][:, 0:512].rearrange("p (a b) -> p a b", a=4), func=AF.Identity),
                    PK(TB[h]), [('b1', tok0 // P)])

        def load_gbc(idx):
            T.dma('sp', ('gbc',), lambda e: e.dma_start(out=gbc[:], in_=gvecs[idx].partition_broadcast(P)),
                  writes=[('gbc',)])

        for s in range(NSEQ):
            r0 = s * S
            with ExitStack() as ar:
                stg = sb("stg", [P, 2, D], F32, ar)
                xnb = sb("xnb", [P, 2, D], BF16, ar)
                gbc = sb("gbc", [P, D], F32, ar)
                sqj = sb("sqj", [P, D], BF16, ar)
                mnT = sb("mnT", [P, KC, NM], BF16, ar)
                tmpA = {'sq': [sb(f"sqa{i}", [P, 512], BF16, ar) for i in range(2)],
                        'rs': [sb(f"rsa{i}", [P, 512], F32, ar) for i in range(2)]}
                load_gbc(0)
                for t in range(NT):
                    sl = t % 2
                    T.dma('sp', ('stg', sl), lambda e, t=t, sl=sl: e.dma_start(out=stg[:, sl, :],
                                                                             in_=x[r0 + t * P:r0 + (t + 1) * P, :]),
                          writes=[('stg', sl)])
                    norm_rows(stg[:, sl, :], [('stg', sl)], gbc, xnb, sl, big1, t * P, sl)
                if stop == 'A0':
                    T.barrier()
                    T.finish()
                    return nc
                load_gbc(1)
                for t in range(2):
                    sl = t % 2
                    T.dma('sp', ('stg', sl), lambda e, t=t, sl=sl: e.dma_start(
                        out=stg[:, sl, :], in_=mem[s * NM + t * P:s * NM + (t + 1) * P, :]), writes=[('stg', sl)])
                    DVE(lambda e, sl=sl: e.memset(ssb[:, sl:sl + 1], 0.0), [], [('ss', sl)])
                    src = stg[:, sl, :]
                    ACT(lambda e, src=src, sl=sl: e.activation(out=sqj[:], in_=src, func=AF.Square,
                                                               accum_out=ssb[:, sl:sl + 1]),
                        [('stg', sl), ('ss', sl)], [('sqj',), ('ss', sl)])
                    ACT(lambda e, sl=sl: e.activation(out=rsb[:, sl:sl + 1], in_=ssb[:, sl:sl + 1], func=AF.Sqrt,
                                                      bias=epsc[:], scale=1.0 / D), [('ss', sl)], [('rsb', sl)])
                    DVE(lambda e, sl=sl: e.reciprocal(out=rsb[:, sl:sl + 1], in_=rsb[:, sl:sl + 1]),
                        [('rsb', sl)], [('rsb', sl)])
                    DVE(lambda e, src=src, sl=sl: e.scalar_tensor_tensor(
                        out=xnb[:, sl, :], in0=src, scalar=rsb[:, sl:sl + 1], in1=gbc[:], op0=ALU.mult,
                        op1=ALU.mult), [('stg', sl), ('rsb', sl), ('gbc',)], [('xnb', sl)])
                    for q in range(4):
                        h = q % 2

                        def f(e, q=q, h=h, sl=sl):
                            for i in range(4):
                                kc = q * 4 + i
                                ins = e.matmul(ps[TB[h]][:, i * 128:(i + 1) * 128], lhsT=xnb[:, sl, kc * 128:(kc + 1) * 128],
                                               rhs=identb[:], start=True, stop=True)
                            return ins
                        PE(f, [('xnb', sl)], PK(TB[h]))
                        ACT(lambda e, q=q, h=h, t=t: e.activation(
                            out=mnT[:, q * 4:(q + 1) * 4, t * P:(t + 1) * P],
                            in_=ps[TB[h]][:, 0:512].rearrange("p (a b) -> p a b", a=4), func=AF.Identity),
                            PK(TB[h]), [('mn',)])
                for h in range(4):
                    slot = wload(wmem[h])
                    bank = gbank()
                    proj_fm(slot, mnT, [('mn',)], 0, bank, n=NM, c0=0)
                    qknorm(bank, pv[:, 3:4], mkT[:, h, :], [('mkT', h)], NM, tmpA, h % 2)
                for h in range(4):
                    slot = wload(wmem[4 + h])
                    bank = gbank()

                    def f(e, slot=slot, bank=bank):
                        for mc in range(2):
                            for kc in range(KC):
                                ins = e.matmul(ps[bank][:, mc * 128:(mc + 1) * 128],
                                               lhsT=mnT[:, kc, mc * P:(mc + 1) * P],
                                               rhs=ring[:, slot, kc * 128:(kc + 1) * 128],
                                               start=(kc == 0), stop=(kc == KC - 1))
                        return ins
                    PE(f, [('ring', slot), ('mn',)], PK(bank, 0, 2))
                    ACT(lambda e, bank=bank, h=h: e.activation(
                        out=mv[:, :, h * 128:(h + 1) * 128],
                        in_=ps[bank][:, 0:256].rearrange("p (a b) -> p a b", a=2), func=AF.Identity),
                        PK(bank, 0, 2), [('mv', h)])
                T.barrier()

            if stop == 'A1':
                T.finish()
                return nc
            with ExitStack() as ar:
                gaT = sb("gaT", [32, S], BF16, ar)
                lav = sb("lav", [P, NT * 256], BF16, ar)
                laf = lav[:].bitcast(F32)
                la_p = laf.rearrange("p (c f) -> p c f", f=128)
                vv = lav[:].rearrange("p (c f) -> p c f", f=256)
                bT = sb("bT", [P, S], F32, ar)
                qT = sb("qT", [P, S], BF16, ar)
                kT = sb("kT", [P, S], BF16, ar)
                ktok = sb("ktok", [P, NT, P], BF16, ar)
                gateT = sb("gateT", [P, 2, S], BF16, ar)
                elast = sb("elast", [P, NT], F32, ar)
                Tst = sb("Tst", [P, P], F32, ar)
                Sbf = sb("Sbf", [P, 2, P], BF16, ar)
                ATm = sb("ATm", [P, 4, P], BF16, ar)
                Etmp = sb("Etmp", [P, 2, 512], F32, ar)
                tmpG = {'sq': [sb(f"sqg{i}", [P, 512], BF16, ar) for i in range(2)],
                        'rs': [sb(f"rsg{i}", [P, 512], F32, ar) for i in range(2)],
                        'lsa': Etmp[:, 0, :], 'lsl': Etmp[:, 1, :]}
                slot = wload(win[WIN_IDX[('small', 0)]])
                for ch in range(NCH):
                    bank = gbank()

                    def f(e, slot=slot, bank=bank, ch=ch):
                        for kc in range(KC):
                            ins = e.matmul(ps[bank][0:32, 0:CW], lhsT=ring[:, slot, kc * 128:kc * 128 + 32],
                                           rhs=big1[:, kc, ch * CW:(ch + 1) * CW], start=(kc == 0),
                                           stop=(kc == KC - 1))
                        return ins
                    PE(f, [('ring', slot)] + [('b1', ch * TPC + i) for i in range(TPC)], PK(bank))
                    ACT(lambda e, bank=bank, ch=ch: e.activation(out=gaT[:, ch * CW:(ch + 1) * CW],
                                                                 in_=ps[bank][0:32, 0:CW], func=AF.Identity),
                        PK(bank), [('gaT', ch)])
                for p in range(4):
                    for g in range(NCH):
                        bank = gbank()

                        def f(e, g=g, bank=bank, p=p):
                            for i in range(TPC):
                                t = g * TPC + i
                                e.matmul(ps[bank][:, i * 128:(i + 1) * 128], lhsT=gaT[0:16, t * P:(t + 1) * P],
                                         rhs=w2b[0:16, p * 128:(p + 1) * 128], start=True, stop=False)
                                ins = e.matmul(ps[bank][:, i * 128:(i + 1) * 128], lhsT=onesf[0:1, :],
                                               rhs=rowps[0:1, p * 128:(p + 1) * 128], start=False, stop=True)
                            return ins
                        PE(f, [('gaT', g)], PK(bank))
                        logsig(ps[bank][:, 0:CW], PK(bank), laf[:, g * CW:(g + 1) * CW],
                               [('lav', g * TPC + i) for i in range(TPC)], CW, tmpG, ('Etmp', 0), ('Etmp', 1))
                    for g in range(NCH):
                        bank = gbank()

                        def f(e, g=g, bank=bank):
                            for i in range(TPC):
                                c = g * TPC + i
                                ins = e.matmul(ps[bank][:, i * 128:(i + 1) * 128], lhsT=la_p[:, c, :], rhs=U16[:],
                                               start=True, stop=True)
                            return ins
                        PE(f, [('lav', g * TPC + i) for i in range(TPC)], PK(bank))
                        ACT(lambda e, g=g, bank=bank: e.activation(out=bT[:, g * CW:(g + 1) * CW],
                                                                   in_=ps[bank][:, 0:CW], func=AF.Identity),
                            PK(bank), [('bT', g)])
                    ACT(lambda e: e.activation(out=elast[:].rearrange("p (c o) -> p c o", o=1),
                                               in_=bT[:].rearrange("p (c t) -> p c t", t=128)[:, :, 127:128],
                                               func=AF.Exp), [('bT', g) for g in range(NCH)], [('elast',)])
                    for (nm, dst, kd, sgn, scl) in (('gq', qT, 'qT', 1.0, 0.125), ('gk', kT, 'kT', -1.0, 1.0)):
                        slot = wload(win[WIN_IDX[(nm, p)]])
                        for ch in range(NCH):
                            bank = gbank()
                            proj_fm(slot, big1, [('b1', ch * TPC + i) for i in range(TPC)], ch, bank)
                            par = ch % 2
                            ACT(lambda e, ch=ch, par=par, sgn=sgn: e.activation(
                                out=Etmp[:, par, 0:CW], in_=bT[:, ch * CW:(ch + 1) * CW], func=AF.Exp, scale=sgn),
                                [('bT', ch)], [('Etmp', par)])
                            DVE(lambda e, ch=ch, par=par, bank=bank, dst=dst, scl=scl: e.scalar_tensor_tensor(
                                out=dst[:, ch * CW:(ch + 1) * CW], in0=ps[bank][:, 0:CW], scalar=scl,
                                in1=Etmp[:, par, 0:CW], op0=ALU.mult, op1=ALU.mult),
                                PK(bank) + [('Etmp', par)], [(kd, ch)])
                    for vb in range(2):
                        slot = wload(win[WIN_IDX[('gv', 2 * p + vb)]])
                        for g in range(NCH):
                            bank = gbank()

                            def f(e, g=g, bank=bank, slot=slot):
                                for i in range(TPC):
                                    t = g * TPC + i
                                    for kc in range(KC):
                                        ins = e.matmul(ps[bank][:, i * 128:(i + 1) * 128],
                                                       lhsT=big1[:, kc, t * P:(t + 1) * P],
                                                       rhs=ring[:, slot, kc * 128:(kc + 1) * 128],
                                                       start=(kc == 0), stop=(kc == KC - 1))
                                return ins
                            PE(f, [('ring', slot)] + [('b1', g * TPC + i) for i in range(TPC)], PK(bank))
                            ACT(lambda e, g=g, bank=bank, vb=vb: e.activation(
                                out=vv[:, g * TPC:(g + 1) * TPC, vb * 128:(vb + 1) * 128],
                                in_=ps[bank][:, 0:CW].rearrange("p (a b) -> p a b", b=128), func=AF.Identity),
                                PK(bank), [('lav', g * TPC + i) for i in range(TPC)])
                    for hh in range(2):
                        slot = wload(win[WIN_IDX[('gg', 2 * p + hh)]])
                        for ch in range(NCH):
                            bank = gbank()
                            proj_fm(slot, big1, [('b1', ch * TPC + i) for i in range(TPC)], ch, bank)
                            ACT(lambda e, ch=ch, bank=bank, hh=hh: e.activation(
                                out=gateT[:, hh, ch * CW:(ch + 1) * CW], in_=ps[bank][:, 0:CW], func=AF.Silu),
                                PK(bank), [('gate', hh, ch)])
                    for g in range(NCH):
                        h = g % 2

                        def f(e, g=g, h=h):
                            for i in range(TPC):
                                c = g * TPC + i
                                ins = e.matmul(ps[TB[h]][:, i * 128:(i + 1) * 128], lhsT=kT[:, c * P:(c + 1) * P],
                                               rhs=identb[:], start=True, stop=True)
                            return ins
                        PE(f, [('kT', g)], PK(TB[h]))
                        ACT(lambda e, g=g, h=h: e.activation(
                            out=ktok[:, g * TPC:(g + 1) * TPC, :],
                            in_=ps[TB[h]][:, 0:CW].rearrange("p (a b) -> p a b", b=128), func=AF.Identity),
                            PK(TB[h]), [('ktok', g)])
                    for c in range(NT):
                        g = c // TPC
                        for hh in range(2):
                            rr = slice(64 * hh, 64 * hh + 64)
                            sl = (c % 2) * 2 + hh
                            cs = slice(c * P, (c + 1) * P)
                            ab, sb_ = 2 + hh, hh
                            PE(lambda e, rr=rr, sl=sl, cs=cs, ab=ab: e.matmul(
                                ps[ab][:, sl * 128:(sl + 1) * 128], lhsT=kT[rr, cs], rhs=qT[rr, cs], start=True,
                                stop=True), [('kT', g), ('qT', g)], PK(ab))
                            DVE(lambda e, sl=sl, ab=ab: e.tensor_tensor(out=ATm[:, sl, :], in0=ps[ab][:, sl * 128:(sl + 1) * 128],
                                                                 in1=Ub[:], op=ALU.mult),
                                PK(ab), [('ATm', sl)])
                            PE(lambda e, sl=sl, c=c, hh=hh, sb_=sb_: e.matmul(
                                ps[sb_][:, sl * 128:(sl + 1) * 128], lhsT=ktok[:, c, :],
                                rhs=vv[:, c, hh * 128:(hh + 1) * 128], start=True, stop=True),
                                [('ktok', g), ('lav', c)], PK(sb_))
                            ob = 4 + hh
                            osl = c % TPC

                            def f(e, sl=sl, c=c, hh=hh, rr=rr, cs=cs, ob=ob, osl=osl):
                                ins = e.matmul(ps[ob][:, osl * 128:(osl + 1) * 128], lhsT=vv[:, c, hh * 128:(hh + 1) * 128],
                                               rhs=ATm[:, sl, :], start=True, stop=(c == 0))
                                if c > 0:
                                    ins = e.matmul(ps[ob][:, osl * 128:(osl + 1) * 128], lhsT=Sbf[rr, (c - 1) % 2, :],
                                                   rhs=qT[rr, cs], start=False, stop=True)
                                return ins
                            PE(f, [('ATm', sl), ('lav', c), ('qT', g)] + ([('Sbf', hh, (c - 1) % 2)] if c > 0 else []),
                               PK(ob, osl, 1))
                            if c == 0:
                                DVE(lambda e, rr=rr, sl=sl, sb_=sb_: e.tensor_copy(out=Tst[rr, :], in_=ps[sb_][rr, sl * 128:(sl + 1) * 128]),
                                    PK(sb_), [('Tst', hh)])
                            else:
                                DVE(lambda e, rr=rr, sl=sl, c=c, sb_=sb_: e.scalar_tensor_tensor(
                                    out=Tst[rr, :], in0=Tst[rr, :], scalar=elast[rr, c - 1:c],
                                    in1=ps[sb_][rr, sl * 128:(sl + 1) * 128], op0=ALU.mult, op1=ALU.add),
                                    PK(sb_) + [('Tst', hh), ('elast',)], [('Tst', hh)])
                            if c < NT - 1:
                                ACT(lambda e, rr=rr, c=c: e.activation(out=Sbf[rr, c % 2, :], in_=Tst[rr, :],
                                                                       func=AF.Identity, scale=elast[rr, c:c + 1]),
                                    [('Tst', hh), ('elast',)], [('Sbf', hh, c % 2)])
                        if c % TPC == TPC - 1:
                            for hh in range(2):
                                epilogue(4 + hh, None, gateT[:, hh, g * CW:(g + 1) * CW], [('gate', hh, g)],
                                         2 * p + hh, g * CW, CW, tmpG, hh)
                T.barrier()

            if stop == 'GLA':
                T.finish()
                return nc
            with ExitStack() as ar:
                fv = sb("fv", [P, NT, 512], BF16, ar)
                fqT = sb("fqT", [P, S], BF16, ar)
                fkT = sb("fkT", [P, S], BF16, ar)
                fgT = sb("fgT", [P, S], BF16, ar)
                lf = sb("lf", [P, NT * 4], F32, ar)
                ctok = sb("ctok", [P, NT * 4], F32, ar)
                cref = sb("cref", [P, NT * 4], F32, ar)
                biasT = sb("biasT", [P, NT, NT], F32, ar)
                PT = sb("PT", [P, 4, P], BF16, ar)
                ltmp = sb("ltmp", [P, 2, 64], F32, ar)
                tmpF = {'sq': [sb(f"sqf{i}", [P, 512], BF16, ar) for i in range(2)],
                        'rs': [sb(f"rsf{i}", [P, 512], F32, ar) for i in range(2)],
                        'rden': [sb(f"rdf{i}", [P, 512], F32, ar) for i in range(2)],
                        'on': [sb(f"onf{i}", [P, 512], F32, ar) for i in range(2)],
                        'lsa': ltmp[:, 0, :], 'lsl': ltmp[:, 1, :]}
                slot = wload(win[WIN_IDX[('small', 0)]])

                def f(e, slot=slot):
                    for t in range(NT):
                        for kc in range(KC):
                            e.matmul(ps[0][:, t * 4:(t + 1) * 4], lhsT=big1[:, kc, t * P:(t + 1) * P],
                                     rhs=ring[:, slot, kc * 128 + 32:kc * 128 + 36], start=(kc == 0), stop=False)
                        ins = e.matmul(ps[0][:, t * 4:(t + 1) * 4], lhsT=onesf[0:1, :], rhs=rowps[0:1, 512:516],
                                       start=False, stop=True)
                    return ins
                PE(f, [('ring', slot)] + [('b1', t) for t in range(NT)], PK(0))
                logsig(ps[0][:, 0:NT * 4], PK(0), lf[:], [('lf',)], NT * 4, tmpF)
                lf3 = lf[:].rearrange("p (b h) -> p b h", h=4)

                def f(e):
                    ins = e.matmul(ps[1][:, 0:NT * 4], lhsT=Uf, rhs=lf[:], start=True, stop=(NT == 1))
                    for b in range(NT - 1):
                        nb = NT - 1 - b
                        ins = e.matmul(ps[1][:, (b + 1) * 4:NT * 4].rearrange("p (b h) -> p b h", h=4), lhsT=onesf,
                                       rhs=lf3[:, b:b + 1, :].to_broadcast([P, nb, 4]),
                                       start=False, stop=(b == NT - 2))
                    return ins
                PE(f, [('lf',)], PK(1))
                ACT(lambda e: e.activation(out=ctok[:], in_=ps[1][:, 0:NT * 4], func=AF.Identity), PK(1), [('ctok',)])
                DVE(lambda e: e.memset(cref[:], 0.0), [], [('cref',)])
                if NT > 1:
                    def f(e):
                        for b in range(NT - 1):
                            nb = NT - 1 - b
                            ins = e.matmul(ps[0][:, (b + 1) * 4:NT * 4].rearrange("p (b h) -> p b h", h=4), lhsT=onesf,
                                           rhs=lf3[:, b:b + 1, :].to_broadcast([P, nb, 4]),
                                           start=(b == 0), stop=(b == NT - 2))
                        return ins
                    PE(f, [('lf',)], PK(0))
                    ACT(lambda e: e.activation(out=cref[:, 4:NT * 4], in_=ps[0][:, 4:NT * 4], func=AF.Identity),
                        PK(0) + [('cref',)], [('cref',)])
                for h in range(4):
                    slot = wload(win[WIN_IDX[('fv', h)]])
                    for g in range(NCH):
                        bank = gbank()

                        def f(e, g=g, bank=bank, slot=slot):
                            for i in range(TPC):
                                t = g * TPC + i
                                for kc in range(KC):
                                    ins = e.matmul(ps[bank][:, i * 128:(i + 1) * 128],
                                                   lhsT=big1[:, kc, t * P:(t + 1) * P],
                                                   rhs=ring[:, slot, kc * 128:(kc + 1) * 128],
                                                   start=(kc == 0), stop=(kc == KC - 1))
                            return ins
                        PE(f, [('ring', slot)] + [('b1', g * TPC + i) for i in range(TPC)], PK(bank))
                        ACT(lambda e, g=g, bank=bank, h=h: e.activation(
                            out=fv[:, g * TPC:(g + 1) * TPC, h * 128:(h + 1) * 128],
                            in_=ps[bank][:, 0:CW].rearrange("p (a b) -> p a b", b=128), func=AF.Identity),
                            PK(bank), [('fv', h)])
                ctok3 = ctok[:].rearrange("p (b h) -> p b h", h=4)
                for h in range(4):
                    for (nm, dst, kd, gi) in (('fq', fqT, 'fqT', 0), ('fk', fkT, 'fkT', 1)):
                        slot = wload(win[WIN_IDX[(nm, h)]])
                        for ch in range(NCH):
                            bank = gbank()
                            proj_fm(slot, big1, [('b1', ch * TPC + i) for i in range(TPC)], ch, bank)
                            qknorm(bank, pv[:, gi:gi + 1], dst[:, ch * CW:(ch + 1) * CW], [(kd, ch)], CW, tmpF, ch % 2)
                    slot = wload(win[WIN_IDX[('fg', h)]])
                    for ch in range(NCH):
                        bank = gbank()
                        proj_fm(slot, big1, [('b1', ch * TPC + i) for i in range(TPC)], ch, bank)
                        ACT(lambda e, ch=ch, bank=bank: e.activation(out=fgT[:, ch * CW:(ch + 1) * CW],
                                                                    in_=ps[bank][:, 0:CW], func=AF.Sigmoid),
                            PK(bank), [('fg', ch)])
                    for ib in range(NT):
                        DVE(lambda e, ib=ib, h=h: e.tensor_scalar(
                            out=biasT[:, ib, 0:ib + 1], in0=ctok3[:, 0:ib + 1, h], scalar1=-1.0,
                            scalar2=cref[:, ib * 4 + h:ib * 4 + h + 1], op0=ALU.mult, op1=ALU.add),
                            [('ctok',), ('cref',)], [('bias', ib)])
                    steps = [(ib, jb) for ib in range(NT) for jb in range(ib + 1)]
                    LA = 2

                    def emit_st(k):
                        ib, jb = steps[k]
                        sl = k % 4
                        stb = (2, 0, 7)[k % 3]
                        PE(lambda e, ib=ib, jb=jb, stb=stb: e.matmul(
                            ps[stb][:, 0:128], lhsT=fkT[:, jb * P:(jb + 1) * P],
                            rhs=fqT[:, ib * P:(ib + 1) * P], start=True, stop=True),
                            [('fkT', jb // TPC), ('fqT', ib // TPC)], PK(stb))
                        ACT(lambda e, ib=ib, jb=jb, sl=sl, stb=stb: e.activation(
                            out=PT[:, sl, :], in_=ps[stb][:, 0:128], func=AF.Exp,
                            bias=biasT[:, ib, jb:jb + 1], scale=128 ** -0.5),
                            PK(stb) + [('bias', ib)], [('PT', sl)])
                        if ib == jb:
                            DVE(lambda e, sl=sl: e.tensor_tensor(out=PT[:, sl, :], in0=PT[:, sl, :], in1=Ub[:],
                                                                 op=ALU.mult), [('PT', sl)], [('PT', sl)])
                    for k in range(min(LA, len(steps))):
                        emit_st(k)
                    for k, (ib, jb) in enumerate(steps):
                        if k + LA < len(steps):
                            emit_st(k + LA)
                        sl = k % 4
                        gg = ib // TPC
                        ob = 4 + gg % 2
                        db = (3, 1)[gg % 2]
                        osl = ib % TPC

                        def f(e, ib=ib, jb=jb, sl=sl, ob=ob, db=db, osl=osl, h=h):
                            e.matmul(ps[ob][:, osl * 128:(osl + 1) * 128], lhsT=fv[:, jb, h * 128:(h + 1) * 128],
                                     rhs=PT[:, sl, :], start=(jb == 0), stop=(jb == ib))
                            return e.matmul(ps[db][:, osl * 128:(osl + 1) * 128], lhsT=onesb[:], rhs=PT[:, sl, :],
                                            start=(jb == 0), stop=(jb == ib))
                        PE(f, [('PT', sl), ('fv', h)], PK(ob, osl, 1) + PK(db, osl, 1))
                        if jb == ib and ib % TPC == TPC - 1:
                            epilogue(ob, db, fgT[:, gg * CW:(gg + 1) * CW], [('fg', gg)], 8 + h, gg * CW, CW, tmpF,
                                     gg % 2)
                T.barrier()

            if stop == 'FOX':
                T.finish()
                return nc
            with ExitStack() as ar:
                mqT = sb("mqT", [P, S], BF16, ar)
                mgT = sb("mgT", [P, S], BF16, ar)
                PTm = sb("PTm", [P, 2, 2, 512], BF16, ar)
                tmpM = {'sq': [sb(f"sqm{i}", [P, 512], BF16, ar) for i in range(2)],
                        'rs': [sb(f"rsm{i}", [P, 512], F32, ar) for i in range(2)],
                        'rden': [sb(f"rdm{i}", [P, 512], F32, ar) for i in range(2)],
                        'on': [sb(f"onm{i}", [P, 512], F32, ar) for i in range(2)]}
                for h in range(4):
                    slot = wload(win[WIN_IDX[('mq', h)]])
                    for ch in range(NCH):
                        bank = gbank()
                        proj_fm(slot, big1, [('b1', ch * TPC + i) for i in range(TPC)], ch, bank)
                        qknorm(bank, pv[:, 2:3], mqT[:, ch * CW:(ch + 1) * CW], [('mqT', ch)], CW, tmpM, ch % 2)
                    slot = wload(win[WIN_IDX[('mg', h)]])
                    for ch in range(NCH):
                        bank = gbank()
                        proj_fm(slot, big1, [('b1', ch * TPC + i) for i in range(TPC)], ch, bank)
                        ACT(lambda e, ch=ch, bank=bank: e.activation(out=mgT[:, ch * CW:(ch + 1) * CW],
                                                                    in_=ps[bank][:, 0:CW], func=AF.Sigmoid),
                            PK(bank), [('mg', ch)])
                    for ch in range(NCH):
                        par = ch % 2
                        for mc in range(2):
                            PE(lambda e, mc=mc, ch=ch, h=h: e.matmul(
                                ps[2 + mc][:, 0:CW], lhsT=mkT[:, h, mc * P:(mc + 1) * P],
                                rhs=mqT[:, ch * CW:(ch + 1) * CW], start=True, stop=True),
                                [('mkT', h), ('mqT', ch)], PK(2 + mc))
                            ACT(lambda e, mc=mc, par=par: e.activation(out=PTm[:, par, mc, 0:CW], in_=ps[2 + mc][:, 0:CW],
                                                                       func=AF.Exp, scale=128 ** -0.5),
                                PK(2 + mc), [('PTm', par, mc)])
                        ob, db = 4 + par, (0, 1)[par]

                        def f(e, par=par, ob=ob, db=db, h=h):
                            for mc in range(2):
                                e.matmul(ps[ob][:, 0:CW], lhsT=mv[:, mc, h * 128:(h + 1) * 128], rhs=PTm[:, par, mc, 0:CW],
                                         start=(mc == 0), stop=(mc == 1))
                            for mc in range(2):
                                ins = e.matmul(ps[db][:, 0:CW], lhsT=onesb[:], rhs=PTm[:, par, mc, 0:CW],
                                               start=(mc == 0), stop=(mc == 1))
                            return ins
                        PE(f, [('PTm', par, 0), ('PTm', par, 1), ('mv', h)], PK(ob) + PK(db))
                        epilogue(ob, db, mgT[:, ch * CW:(ch + 1) * CW], [('mg', ch)], 12 + h, ch * CW, CW, tmpM, par)
                T.barrier()

            if stop == 'MEM':
                T.finish()
                return nc
            with ExitStack() as ar:
                hst = sb("hst", [P, TPC, D], F32, ar)
                xnb = sb("xnb2", [P, 2, D], BF16, ar)
                gbc = sb("gbc2", [P, D], F32, ar)
                sqj = sb("sqj2", [P, D], BF16, ar)
                load_gbc(2)
                for tt in range(NCH):
                    for sub in range(TPC):
                        t = tt * TPC + sub
                        T.dma('sp', ('hst', sub), lambda e, t=t, sub=sub: e.dma_start(
                            out=hst[:, sub, :], in_=x[r0 + t * P:r0 + (t + 1) * P, :]), writes=[('hst', sub)])
                    for j in range(16):
                        slot = wload(wout[j])
                        bank = gbank((0, 1, 2, 3))

                        def f(e, slot=slot, bank=bank, tt=tt):
                            for sub in range(TPC):
                                t = tt * TPC + sub
                                for hc in range(16):
                                    ins = e.matmul(ps[bank][:, sub * 128:(sub + 1) * 128],
                                                   lhsT=ogT[:, hc, t * P:(t + 1) * P],
                                                   rhs=ring[:, slot, hc * 128:(hc + 1) * 128],
                                                   start=(hc == 0), stop=(hc == 15))
                            return ins
                        PE(f, [('ring', slot)] + [('og', hc, tt) for hc in range(16)], PK(bank))
                        DVE(lambda e, bank=bank, j=j: e.tensor_tensor(
                            out=hst[:, :, j * 128:(j + 1) * 128], in0=ps[bank][:, 0:CW].rearrange("p (a b) -> p a b", b=128),
                            in1=hst[:, :, j * 128:(j + 1) * 128], op=ALU.add),
                            PK(bank) + [('hst', sub) for sub in range(TPC)], [('hst', sub) for sub in range(TPC)])
                    for sub in range(TPC):
                        t = tt * TPC + sub
                        norm_rows(hst[:, sub, :], [('hst', sub)], gbc, xnb, sub % 2, big1, t * P, sub % 2)
                        T.dma('sp', ('hout', sub), lambda e, t=t, sub=sub: e.dma_start(
                            out=out[r0 + t * P:r0 + (t + 1) * P, :], in_=hst[:, sub, :]),
                            reads=[('hst', sub)], writes=[('outrow', s, t)], is_out=True)
                T.barrier()

            if stop == 'B1':
                T.finish()
                return nc
            with ExitStack() as ar:
                hres = sb("hres", [P, 2, TPC, 512], F32, ar)
                rtmp = sb("rtmp", [P, 2, 512], F32, ar)
                for tt in range(NCH):
                    c0 = tt * CW
                    for f_ in range(64):
                        slot = wload(wup[f_])
                        bank = gbank((4, 5, 6))
                        proj_fm(slot, big1, [('b1', tt * TPC + i) for i in range(TPC)], tt, bank)
                        par = f_ % 2
                        ACT(lambda e, bank=bank, par=par: e.activation(out=rtmp[:, par, 0:CW], in_=ps[bank][:, 0:CW],
                                                                       func=AF.Relu), PK(bank), [('rtmp', par)])
                        DVE(lambda e, par=par, f_=f_: e.tensor_tensor(out=uT[:, f_, 0:CW], in0=rtmp[:, par, 0:CW],
                                                                       in1=rtmp[:, par, 0:CW], op=ALU.mult),
                            [('rtmp', par)], [('uT', f_)])
                    for j in range(4):
                        hp = j % 2
                        for sub in range(TPC):
                            t = tt * TPC + sub
                            T.dma('sp', ('hres', hp, sub), lambda e, t=t, sub=sub, hp=hp, j=j: e.dma_start(
                                out=hres[:, hp, sub, :], in_=out[r0 + t * P:r0 + (t + 1) * P, j * 512:(j + 1) * 512]),
                                reads=[('outrow', s, t)], writes=[('hres', hp, sub)])
                        for fg in range(16):
                            slot = wload(wdn[j * 16 + fg])

                            def f(e, slot=slot, fg=fg):
                                for fi in range(4):
                                    ff = fg * 4 + fi
                                    for sub in range(TPC):
                                        ins = e.matmul(ps[sub][:, :], lhsT=uT[:, ff, sub * P:(sub + 1) * P],
                                                       rhs=ring[:, slot, fi * 512:(fi + 1) * 512],
                                                       start=(ff == 0), stop=(ff == 63))
                                return ins
                            PE(f, [('ring', slot)] + [('uT', fg * 4 + fi) for fi in range(4)],
                               [k for sub in range(TPC) for k in PK(sub)])
                        for sub in range(TPC):
                            t = tt * TPC + sub
                            DVE(lambda e, sub=sub, hp=hp: e.tensor_tensor(out=hres[:, hp, sub, :], in0=ps[sub][:, :],
                                                                         in1=hres[:, hp, sub, :], op=ALU.add),
                                PK(sub) + [('hres', hp, sub)], [('hres', hp, sub)])
                            T.dma('sp', ('yout', hp, sub), lambda e, t=t, sub=sub, hp=hp, j=j: e.dma_start(
                                out=out[r0 + t * P:r0 + (t + 1) * P, j * 512:(j + 1) * 512], in_=hres[:, hp, sub, :]),
                                reads=[('hres', hp, sub)], writes=[('outfin', s, t, j)], is_out=True)
                T.barrier()
        T.finish()
    return nc


def _host_layouts(inp):
    f32 = np.float32
    w_in = np.asarray(inp["w_in"], f32)
    blocks = []
    for (nm, i) in WIN_BLOCKS:
        if nm == 'small':
            blk = np.zeros((D, 128), f32)
            blk[:, 0:16] = w_in[:, 3072:3088]
            blk[:, 32:36] = w_in[:, 5136:5140]
        else:
            c = win_cols(nm, i)
            blk = w_in[:, c[0]:c[-1] + 1]
        blocks.append(blk.reshape(KC, P, 128).transpose(1, 0, 2).reshape(P, 2048))
    win = np.ascontiguousarray(np.stack(blocks))

    def colblocks(w, nb):
        K = w.shape[0] // P
        return np.ascontiguousarray(
            w.reshape(K, P, nb, 128).transpose(2, 1, 0, 3).reshape(nb, P, K * 128))

    wmem = colblocks(np.asarray(inp["w_mem_kv"], f32), 8)
    wout = colblocks(np.asarray(inp["w_out"], f32), 16)
    wup = colblocks(np.asarray(inp["w_up"], f32), 64)
    wd = np.asarray(inp["w_down"], f32)
    wdn = np.ascontiguousarray(
        wd.reshape(16, 4, P, 4, 512).transpose(3, 0, 2, 1, 4).reshape(64, P, 2048))
    consts = np.zeros((P, 384), f32)
    consts[:, 0:128] = np.eye(P, dtype=f32)
    consts[:, 128:256] = np.triu(np.ones((P, P), f32))
    consts[:, 256:384] = 1.0
    pvec = np.zeros((P, 20), f32)
    pvec[:, 0] = inp["fox_q_norm_g"]
    pvec[:, 1] = inp["fox_k_norm_g"]
    pvec[:, 2] = inp["mem_q_norm_g"]
    pvec[:, 3] = inp["mem_k_norm_g"]
    pvec[:, 4:20] = np.asarray(inp["out_norm_g"], f32).reshape(16, P).T
    gvecs = np.stack([inp["attn_norm_g"], inp["mem_norm_g"], inp["mlp_norm_g"]]).astype(f32)
    rowp = np.concatenate([inp["gla_a_b"], inp["fox_f_b"]]).astype(f32)[None, :]
    return dict(consts=consts, pvec=pvec, gvecs=np.ascontiguousarray(gvecs),
                w2=np.ascontiguousarray(inp["gla_a_w2"], dtype=f32), rowp=np.ascontiguousarray(rowp),
                win=win, wmem=wmem, wout=wout, wup=wup, wdn=wdn)


def run(inputs, n_cores, trace=False):
    x = np.asarray(inputs["x"], np.float32)
    mem = np.asarray(inputs["mem"], np.float32)
    B, S, _ = x.shape
    nseq = B // n_cores
    shared = _host_layouts(inputs)
    import os
    nc = build(nseq, S, os.environ.get('KSTOP'))
    in_maps = []
    for c in range(n_cores):
        m = dict(shared)
        m["x"] = np.ascontiguousarray(x[c * nseq:(c + 1) * nseq].reshape(nseq * S, D))
        m["mem"] = np.ascontiguousarray(mem[c * nseq:(c + 1) * nseq].reshape(nseq * 256, D))
        in_maps.append(m)
    res = run_bass_kernel_spmd(nc, in_maps, core_ids=list(range(n_cores)), trace=trace)
    outs = [r["out"].reshape(nseq, S, D) for r in res.results]
    return np.concatenate(outs, axis=0).astype(np.float32), res


def kernel(**inputs):
    y, _ = run(inputs, 8)
    return y
```

```python
import numpy as np
from contextlib import ExitStack
import concourse.bass as bass
import concourse.mybir as mybir
from concourse.bass_utils import run_bass_kernel_spmd

F32 = mybir.dt.float32
BF16 = mybir.dt.bfloat16
AF = mybir.ActivationFunctionType
ALU = mybir.AluOpType
P = 128
D = 2048
KC = 16
EPS = 1e-6
NRING = 3


class Tracker:
    def __init__(self, nc, st):
        self.nc = nc
        self.st = st
        self.eng = {'pe': nc.tensor, 'act': nc.scalar, 'dve': nc.vector, 'pool': nc.gpsimd, 'sp': nc.sync}
        self.sem = {e: st.enter_context(nc.semaphore('s_' + e)) for e in ('pe', 'act', 'dve')}
        self.cnt = {e: 0 for e in self.sem}
        self.seen = {e: {} for e in self.eng}
        self.lw = {}
        self.rd = {}
        self.dsem = {}
        self.dcnt = {}
        self.out_toks = {}

    def _wait(self, eng, toks):
        need = {}
        seen = self.seen[eng]
        for (key, sem, val) in toks:
            if seen.get(key, 0) >= val:
                continue
            if key not in need or need[key][1] < val:
                need[key] = (sem, val)
        for key, (sem, val) in need.items():
            self.eng[eng].wait_ge(sem, val)
            seen[key] = val

    def _collect(self, eng, reads, writes):
        toks = []
        for r in reads:
            t = self.lw.get(r)
            if t is not None and not (t[3] == eng and eng == 'pe'):
                toks.append(t[:3])
        for w in writes:
            t = self.lw.get(w)
            if t is not None and (t[3] != eng or eng is None):
                toks.append(t[:3])
            for t in self.rd.get(w, ()):
                if t[3] != eng or eng is None:
                    toks.append(t[:3])
        return toks

    def _commit(self, tok, reads, writes):
        for w in writes:
            self.lw[w] = tok
            self.rd[w] = []
        for r in reads:
            self.rd.setdefault(r, []).append(tok)

    def op(self, eng, fn, reads=(), writes=()):
        psr = [r for r in reads if r[0] == 'ps']
        if psr:
            reads = [r for r in reads if r[0] != 'ps']
            writes = list(writes) + [r for r in psr if r not in writes]
        self._wait(eng, self._collect(eng, reads, writes))
        ins = fn(self.eng[eng])
        self.cnt[eng] += 1
        ins.then_inc(self.sem[eng], 1)
        self._commit((eng, self.sem[eng], self.cnt[eng], eng), reads, writes)

    def dma(self, q, key, fn, reads=(), writes=(), is_out=False):
        self._wait(q, self._collect(None, reads, writes))
        if key not in self.dsem:
            self.dsem[key] = self.st.enter_context(self.nc.semaphore('d_' + '_'.join(str(k) for k in key)))
            self.dcnt[key] = 0
        ins = fn(self.eng[q])
        self.dcnt[key] += 16
        ins.then_inc(self.dsem[key], 16)
        tok = (('d',) + tuple(key), self.dsem[key], self.dcnt[key], None)
        self._commit(tok, reads, writes)
        if is_out:
            self.out_toks[key] = tok

    def barrier(self):
        toks = [(e, self.sem[e], self.cnt[e]) for e in self.sem if self.cnt[e]]
        toks += [(('d',) + tuple(k), self.dsem[k], self.dcnt[k]) for k in self.dsem if k[0] != 'ring']
        for e in ('pe', 'act', 'dve', 'sp'):
            self._wait(e, toks)

    def finish(self):
        self._wait('sp', [t[:3] for t in self.out_toks.values()])


def win_cols(name, i):
    base = {'gq': 0, 'gk': 512, 'gv': 1024, 'gg': 2048, 'fq': 3088, 'fk': 3600, 'fv': 4112,
            'fg': 4624, 'mq': 5140, 'mg': 5652}[name]
    return list(range(base + 128 * i, base + 128 * i + 128))


def win_block_list():
    blocks = [('small', 0)]
    for nm, n in (('gq', 4), ('gk', 4), ('gv', 8), ('gg', 8), ('fq', 4), ('fk', 4), ('fv', 4), ('fg', 4),
                  ('mq', 4), ('mg', 4)):
        blocks += [(nm, i) for i in range(n)]
    return blocks


WIN_BLOCKS = win_block_list()
WIN_IDX = {b: i for i, b in enumerate(WIN_BLOCKS)}


def build(NSEQ, S, stop=None):
    NT = S // P
    CW = min(512, S)
    NCH = S // CW
    TPC = CW // P
    NM = 256
    nc = bass.Bass("TRN2", target_bir_lowering=False)

    def din(name, shape):
        return nc.dram_tensor(name, list(shape), F32, kind="ExternalInput").ap()

    x = din("x", [NSEQ * S, D])
    mem = din("mem", [NSEQ * NM, D])
    consts = din("consts", [P, 384])
    pvec = din("pvec", [P, 20])
    gvecs = din("gvecs", [3, D])
    w2d = din("w2", [16, 512])
    rowp = din("rowp", [1, 516])
    win = din("win", [49, P, 2048])
    wmem = din("wmem", [8, P, 2048])
    wout = din("wout", [16, P, 2048])
    wup = din("wup", [64, P, 2048])
    wdn = din("wdn", [64, P, 2048])
    out = nc.dram_tensor("out", [NSEQ * S, D], F32, kind="ExternalOutput").ap()

    with ExitStack() as st:
        T = Tracker(nc, st)

        uid = [0]

        def sb(name, shape, dt, stack=st):
            uid[0] += 1
            return stack.enter_context(nc.sbuf_tensor(f"{name}_{uid[0]}", list(shape), dt))

        ps = [st.enter_context(nc.psum_tensor(f"ps{i}", [P, 512], F32)) for i in range(8)]
        TB = (7, 6)

        def PK(b, lo=0, n=4):
            return [('ps', b)]

        def ACT(fn, r=(), w=()):
            T.op('act', fn, r, w)

        def DVE(fn, r=(), w=()):
            T.op('dve', fn, r, w)

        def PE(fn, r=(), w=()):
            T.op('pe', fn, r, w)

        big1 = sb("big1", [P, KC, S], BF16)
        big2 = sb("big2", [P, max(16 * S, 64 * CW)], BF16)
        ogT = big2[:, 0:16 * S].rearrange("p (h s) -> p h s", h=16)
        uT = big2[:, 0:64 * CW].rearrange("p (f t) -> p f t", f=64)
        ring = sb("ring", [P, NRING, 2048], BF16)
        cf = sb("cf", [P, 384], F32)
        identb = sb("identb", [P, P], BF16)
        Ub = sb("Ub", [P, P], BF16)
        onesb = sb("onesb", [P, P], BF16)
        onesDb = sb("onesDb", [P, P], BF16)
        U16 = sb("U16", [P, P], F32)
        pv = sb("pv", [P, 20], F32)
        w2f = sb("w2f", [16, 512], F32)
        w2b = sb("w2b", [16, 512], BF16)
        rowps = sb("rowps", [1, 516], F32)
        epsc = sb("epsc", [P, 1], F32)
        mkT = sb("mkT", [P, 4, NM], BF16)
        mv = sb("mv", [P, 2, 512], BF16)
        ssb = sb("ssb", [P, 8], F32)
        rsb = sb("rsb", [P, 8], F32)
        identf = cf[:, 0:128]
        Uf = cf[:, 128:256]
        onesf = cf[:, 256:384]

        T.dma('sp', ('c', 0), lambda e: e.dma_start(out=cf[:], in_=consts), writes=[('cf',)])
        T.dma('sp', ('c', 1), lambda e: e.dma_start(out=pv[:], in_=pvec), writes=[('pv',)])
        T.dma('sp', ('c', 2), lambda e: e.dma_start(out=w2f[:], in_=w2d), writes=[('w2f',)])
        T.dma('sp', ('c', 3), lambda e: e.dma_start(out=rowps[:], in_=rowp), writes=[('rowps',)])
        DVE(lambda e: e.tensor_copy(out=identb[:], in_=identf), [('cf',)], [('k1',)])
        DVE(lambda e: e.tensor_copy(out=Ub[:], in_=Uf), [('cf',)], [('k2',)])
        DVE(lambda e: e.tensor_copy(out=onesb[:], in_=onesf), [('cf',)], [('k3',)])
        DVE(lambda e: e.tensor_scalar(out=onesDb[:], in0=onesf, scalar1=1.0 / 128, scalar2=None, op0=ALU.mult),
            [('cf',)], [('k4',)])
        DVE(lambda e: e.tensor_scalar(out=U16[:], in0=Uf, scalar1=1.0 / 16, scalar2=None, op0=ALU.mult),
            [('cf',)], [('k5',)])
        DVE(lambda e: e.tensor_copy(out=w2b[:], in_=w2f[:]), [('w2f',)], [('k6',)])
        DVE(lambda e: e.memset(epsc[:], EPS), [], [('k7',)])
        T.barrier()
        if stop == 'INIT':
            T.finish()
            return nc

        wcount = [0]

        def wload(src):
            slot = wcount[0] % NRING
            wcount[0] += 1
            T.dma('pool', ('ring', slot), lambda e: e.dma_start(out=ring[:, slot, :], in_=src),
                  writes=[('ring', slot)])
            return slot

        rot = [0]

        def gbank(banks=(0, 1)):
            b = banks[rot[0] % len(banks)]
            rot[0] += 1
            return b

        def proj_fm(slot, srcT, ksrc, ch, bank, n=None, c0=None):
            n = CW if n is None else n
            c0 = ch * CW if c0 is None else c0

            def f(e):
                for kc in range(KC):
                    ins = e.matmul(ps[bank][:, 0:n], lhsT=ring[:, slot, kc * 128:(kc + 1) * 128],
                                   rhs=srcT[:, kc, c0:c0 + n], start=(kc == 0), stop=(kc == KC - 1))
                return ins
            PE(f, [('ring', slot)] + ksrc, PK(bank))

        def rms_cols(src, ksrc, n, tmp, par):
            sq, rs = tmp['sq'][par], tmp['rs'][par]
            ACT(lambda e: e.activation(out=sq[:, 0:n], in_=src, func=AF.Square), ksrc, [('sq', par)])
            PE(lambda e: e.matmul(ps[6][:, 0:n], lhsT=onesDb[:], rhs=sq[:, 0:n], start=True, stop=True),
               [('sq', par)], PK(6))
            ACT(lambda e: e.activation(out=rs[:, 0:n], in_=ps[6][:, 0:n], func=AF.Sqrt, bias=epsc[:], scale=1.0),
                PK(6), [('rs', par)])
            DVE(lambda e: e.reciprocal(out=rs[:, 0:n], in_=rs[:, 0:n]), [('rs', par)], [('rs', par)])
            return rs

        def qknorm(bank, gain, dst, kdst, n, tmp, par):
            rs = rms_cols(ps[bank][:, 0:n], PK(bank), n, tmp, par)
            DVE(lambda e: e.scalar_tensor_tensor(out=dst, in0=ps[bank][:, 0:n], scalar=gain, in1=rs[:, 0:n],
                                                 op0=ALU.mult, op1=ALU.mult),
                PK(bank) + [('rs', par)], kdst)

        def epilogue(obank, dbank, gate, kgate, head, t0, n, tmp, par):
            if dbank is not None:
                rden, on = tmp['rden'][par], tmp['on'][par]
                DVE(lambda e: e.reciprocal(out=rden[:, 0:n], in_=ps[dbank][:, 0:n]), PK(dbank), [('rden', par)])
                DVE(lambda e: e.tensor_tensor(out=on[:, 0:n], in0=ps[obank][:, 0:n], in1=rden[:, 0:n], op=ALU.mult),
                    PK(obank) + [('rden', par)], [('on', par)])
                src, ksrc = on[:, 0:n], [('on', par)]
            else:
                src, ksrc = ps[obank][:, 0:n], PK(obank)
            rs = rms_cols(src, ksrc, n, tmp, par)
            DVE(lambda e: e.tensor_tensor(out=rs[:, 0:n], in0=rs[:, 0:n], in1=gate, op=ALU.mult),
                [('rs', par)] + kgate, [('rs', par)])
            DVE(lambda e: e.scalar_tensor_tensor(out=ogT[:, head, t0:t0 + n], in0=src,
                                                 scalar=pv[:, 4 + head:5 + head], in1=rs[:, 0:n],
                                                 op0=ALU.mult, op1=ALU.mult),
                ksrc + [('rs', par)], [('og', head, t0 // CW)])

        def logsig(zsrc, kz, dst, kdst, n, tmp, ka=('lsa',), kl=('lsl',)):
            a, l = tmp['lsa'], tmp['lsl']
            ACT(lambda e: e.activation(out=a[:, 0:n], in_=zsrc, func=AF.Abs), kz, [ka])
            ACT(lambda e: e.activation(out=a[:, 0:n], in_=a[:, 0:n], func=AF.Exp, scale=-1.0), [ka], [ka])
            ACT(lambda e: e.activation(out=l[:, 0:n], in_=a[:, 0:n], func=AF.Ln, bias=1.0, scale=1.0),
                [ka], [kl])
            DVE(lambda e: e.scalar_tensor_tensor(out=dst, in0=zsrc, scalar=0.0, in1=l[:, 0:n],
                                                 op0=ALU.min, op1=ALU.subtract),
                kz + [kl], kdst)

        def norm_rows(src, ksrc, gbc, xnb, par, dstT, tok0, cell):
            DVE(lambda e: e.memset(ssb[:, cell:cell + 1], 0.0), [], [('ss', cell)])
            ACT(lambda e: e.activation(out=sqj[:], in_=src, func=AF.Square, accum_out=ssb[:, cell:cell + 1]),
                ksrc + [('ss', cell)], [('sqj',), ('ss', cell)])
            ACT(lambda e: e.activation(out=rsb[:, cell:cell + 1], in_=ssb[:, cell:cell + 1], func=AF.Sqrt,
                                       bias=epsc[:], scale=1.0 / D), [('ss', cell)], [('rsb', cell)])
            DVE(lambda e: e.reciprocal(out=rsb[:, cell:cell + 1], in_=rsb[:, cell:cell + 1]),
                [('rsb', cell)], [('rsb', cell)])
            DVE(lambda e: e.scalar_tensor_tensor(out=xnb[:, par, :], in0=src, scalar=rsb[:, cell:cell + 1],
                                                 in1=gbc[:], op0=ALU.mult, op1=ALU.mult),
                ksrc + [('rsb', cell), ('gbc',)], [('xnb', par)])

            def stage2():
                for q in range(4):
                    h = q % 2

                    def f(e, q=q, h=h):
                        for i in range(4):
                            kc = q * 4 + i
                            ins = e.matmul(ps[TB[h]][:, i * 128:(i + 1) * 128],
                                           lhsT=xnb[:, par, kc * 128:(kc + 1) * 128],
                                           rhs=identb[:], start=True, stop=True)
                        return ins
                    PE(f, [('xnb', par)], PK(TB[h]))
                    ACT(lambda e, q=q, h=h: e.activation(
                        out=dstT[:, q * 4:(q + 1) * 4, tok0:tok0 + P],
                        in_=ps[TB[h]][:, 0:512].rearrange("p (a b) -> p a b", a=4), func=AF.Identity),
                        PK(TB[h]), [('b1', tok0 // P)])
            return stage2

        def load_gbc(idx):
            T.dma('sp', ('gbc',), lambda e: e.dma_start(out=gbc[:], in_=gvecs[idx].partition_broadcast(P)),
                  writes=[('gbc',)])

        for s in range(NSEQ):
            r0 = s * S
            with ExitStack() as ar:
                stg = sb("stg", [P, 2, D], F32, ar)
                xnb = sb("xnb", [P, 2, D], BF16, ar)
                gbc = sb("gbc", [P, D], F32, ar)
                sqj = sb("sqj", [P, D], BF16, ar)
                mnT = sb("mnT", [P, KC, NM], BF16, ar)
                tmpA = {'sq': [sb(f"sqa{i}", [P, 512], BF16, ar) for i in range(2)],
                        'rs': [sb(f"rsa{i}", [P, 512], F32, ar) for i in range(2)]}
                load_gbc(0)
                pend_s2 = None
                for t in range(NT):
                    sl = t % 2
                    T.dma('sp', ('stg', sl), lambda e, t=t, sl=sl: e.dma_start(out=stg[:, sl, :],
                                                                             in_=x[r0 + t * P:r0 + (t + 1) * P, :]),
                          writes=[('stg', sl)])
                    s2 = norm_rows(stg[:, sl, :], [('stg', sl)], gbc, xnb, sl, big1, t * P, sl)
                    if pend_s2 is not None:
                        pend_s2()
                    pend_s2 = s2
                pend_s2()
                if stop == 'A0':
                    T.barrier()
                    T.finish()
                    return nc
                load_gbc(1)
                for t in range(2):
                    sl = t % 2
                    T.dma('sp', ('stg', sl), lambda e, t=t, sl=sl: e.dma_start(
                        out=stg[:, sl, :], in_=mem[s * NM + t * P:s * NM + (t + 1) * P, :]), writes=[('stg', sl)])
                    DVE(lambda e, sl=sl: e.memset(ssb[:, sl:sl + 1], 0.0), [], [('ss', sl)])
                    src = stg[:, sl, :]
                    ACT(lambda e, src=src, sl=sl: e.activation(out=sqj[:], in_=src, func=AF.Square,
                                                               accum_out=ssb[:, sl:sl + 1]),
                        [('stg', sl), ('ss', sl)], [('sqj',), ('ss', sl)])
                    ACT(lambda e, sl=sl: e.activation(out=rsb[:, sl:sl + 1], in_=ssb[:, sl:sl + 1], func=AF.Sqrt,
                                                      bias=epsc[:], scale=1.0 / D), [('ss', sl)], [('rsb', sl)])
                    DVE(lambda e, sl=sl: e.reciprocal(out=rsb[:, sl:sl + 1], in_=rsb[:, sl:sl + 1]),
                        [('rsb', sl)], [('rsb', sl)])
                    DVE(lambda e, src=src, sl=sl: e.scalar_tensor_tensor(
                        out=xnb[:, sl, :], in0=src, scalar=rsb[:, sl:sl + 1], in1=gbc[:], op0=ALU.mult,
                        op1=ALU.mult), [('stg', sl), ('rsb', sl), ('gbc',)], [('xnb', sl)])
                    for q in range(4):
                        h = q % 2

                        def f(e, q=q, h=h, sl=sl):
                            for i in range(4):
                                kc = q * 4 + i
                                ins = e.matmul(ps[TB[h]][:, i * 128:(i + 1) * 128], lhsT=xnb[:, sl, kc * 128:(kc + 1) * 128],
                                               rhs=identb[:], start=True, stop=True)
                            return ins
                        PE(f, [('xnb', sl)], PK(TB[h]))
                        ACT(lambda e, q=q, h=h, t=t: e.activation(
                            out=mnT[:, q * 4:(q + 1) * 4, t * P:(t + 1) * P],
                            in_=ps[TB[h]][:, 0:512].rearrange("p (a b) -> p a b", a=4), func=AF.Identity),
                            PK(TB[h]), [('mn',)])
                for h in range(4):
                    slot = wload(wmem[h])
                    bank = gbank()
                    proj_fm(slot, mnT, [('mn',)], 0, bank, n=NM, c0=0)
                    qknorm(bank, pv[:, 3:4], mkT[:, h, :], [('mkT', h)], NM, tmpA, h % 2)
                for h in range(4):
                    slot = wload(wmem[4 + h])
                    bank = gbank()

                    def f(e, slot=slot, bank=bank):
                        for mc in range(2):
                            for kc in range(KC):
                                ins = e.matmul(ps[bank][:, mc * 128:(mc + 1) * 128],
                                               lhsT=mnT[:, kc, mc * P:(mc + 1) * P],
                                               rhs=ring[:, slot, kc * 128:(kc + 1) * 128],
                                               start=(kc == 0), stop=(kc == KC - 1))
                        return ins
                    PE(f, [('ring', slot), ('mn',)], PK(bank, 0, 2))
                    ACT(lambda e, bank=bank, h=h: e.activation(
                        out=mv[:, :, h * 128:(h + 1) * 128],
                        in_=ps[bank][:, 0:256].rearrange("p (a b) -> p a b", a=2), func=AF.Identity),
                        PK(bank, 0, 2), [('mv', h)])
                T.barrier()

            if stop == 'A1':
                T.finish()
                return nc
            with ExitStack() as ar:
                gaT = sb("gaT", [32, S], BF16, ar)
                lav = sb("lav", [P, NT * 256], BF16, ar)
                laf = lav[:].bitcast(F32)
                la_p = laf.rearrange("p (c f) -> p c f", f=128)
                vv = lav[:].rearrange("p (c f) -> p c f", f=256)
                bT = sb("bT", [P, S], F32, ar)
                qT = sb("qT", [P, S], BF16, ar)
                kT = sb("kT", [P, S], BF16, ar)
                ktok = sb("ktok", [P, NT, P], BF16, ar)
                gateT = sb("gateT", [P, 2, S], BF16, ar)
                elast = sb("elast", [P, NT], F32, ar)
                Tst = sb("Tst", [P, P], F32, ar)
                Sbf = sb("Sbf", [P, 2, P], BF16, ar)
                ATm = sb("ATm", [P, 4, P], BF16, ar)
                Etmp = sb("Etmp", [P, 2, 512], F32, ar)
                tmpG = {'sq': [sb(f"sqg{i}", [P, 512], BF16, ar) for i in range(2)],
                        'rs': [sb(f"rsg{i}", [P, 512], F32, ar) for i in range(2)],
                        'lsa': Etmp[:, 0, :], 'lsl': Etmp[:, 1, :]}
                slot = wload(win[WIN_IDX[('small', 0)]])
                for ch in range(NCH):
                    bank = gbank()

                    def f(e, slot=slot, bank=bank, ch=ch):
                        for kc in range(KC):
                            ins = e.matmul(ps[bank][0:32, 0:CW], lhsT=ring[:, slot, kc * 128:kc * 128 + 32],
                                           rhs=big1[:, kc, ch * CW:(ch + 1) * CW], start=(kc == 0),
                                           stop=(kc == KC - 1))
                        return ins
                    PE(f, [('ring', slot)] + [('b1', ch * TPC + i) for i in range(TPC)], PK(bank))
                    ACT(lambda e, bank=bank, ch=ch: e.activation(out=gaT[:, ch * CW:(ch + 1) * CW],
                                                                 in_=ps[bank][0:32, 0:CW], func=AF.Identity),
                        PK(bank), [('gaT', ch)])
                for p in range(4):
                    for g in range(NCH):
                        bank = gbank()

                        def f(e, g=g, bank=bank, p=p):
                            for i in range(TPC):
                                t = g * TPC + i
                                e.matmul(ps[bank][:, i * 128:(i + 1) * 128], lhsT=gaT[0:16, t * P:(t + 1) * P],
                                         rhs=w2b[0:16, p * 128:(p + 1) * 128], start=True, stop=False)
                                ins = e.matmul(ps[bank][:, i * 128:(i + 1) * 128], lhsT=onesf[0:1, :],
                                               rhs=rowps[0:1, p * 128:(p + 1) * 128], start=False, stop=True)
                            return ins
                        PE(f, [('gaT', g)], PK(bank))
                        logsig(ps[bank][:, 0:CW], PK(bank), laf[:, g * CW:(g + 1) * CW],
                               [('lav', g * TPC + i) for i in range(TPC)], CW, tmpG, ('Etmp', 0), ('Etmp', 1))
                    for g in range(NCH):
                        bank = gbank()

                        def f(e, g=g, bank=bank):
                            for i in range(TPC):
                                c = g * TPC + i
                                ins = e.matmul(ps[bank][:, i * 128:(i + 1) * 128], lhsT=la_p[:, c, :], rhs=U16[:],
                                               start=True, stop=True)
                            return ins
                        PE(f, [('lav', g * TPC + i) for i in range(TPC)], PK(bank))
                        ACT(lambda e, g=g, bank=bank: e.activation(out=bT[:, g * CW:(g + 1) * CW],
                                                                   in_=ps[bank][:, 0:CW], func=AF.Identity),
                            PK(bank), [('bT', g)])
                    ACT(lambda e: e.activation(out=elast[:].rearrange("p (c o) -> p c o", o=1),
                                               in_=bT[:].rearrange("p (c t) -> p c t", t=128)[:, :, 127:128],
                                               func=AF.Exp), [('bT', g) for g in range(NCH)], [('elast',)])
                    for (nm, dst, kd, sgn, scl) in (('gq', qT, 'qT', 1.0, 0.125), ('gk', kT, 'kT', -1.0, 1.0)):
                        slot = wload(win[WIN_IDX[(nm, p)]])
                        for ch in range(NCH):
                            bank = gbank()
                            proj_fm(slot, big1, [('b1', ch * TPC + i) for i in range(TPC)], ch, bank)
                            par = ch % 2
                            ACT(lambda e, ch=ch, par=par, sgn=sgn: e.activation(
                                out=Etmp[:, par, 0:CW], in_=bT[:, ch * CW:(ch + 1) * CW], func=AF.Exp, scale=sgn),
                                [('bT', ch)], [('Etmp', par)])
                            DVE(lambda e, ch=ch, par=par, bank=bank, dst=dst, scl=scl: e.scalar_tensor_tensor(
                                out=dst[:, ch * CW:(ch + 1) * CW], in0=ps[bank][:, 0:CW], scalar=scl,
                                in1=Etmp[:, par, 0:CW], op0=ALU.mult, op1=ALU.mult),
                                PK(bank) + [('Etmp', par)], [(kd, ch)])
                    for vb in range(2):
                        slot = wload(win[WIN_IDX[('gv', 2 * p + vb)]])
                        for g in range(NCH):
                            bank = gbank()

                            def f(e, g=g, bank=bank, slot=slot):
                                for i in range(TPC):
                                    t = g * TPC + i
                                    for kc in range(KC):
                                        ins = e.matmul(ps[bank][:, i * 128:(i + 1) * 128],
                                                       lhsT=big1[:, kc, t * P:(t + 1) * P],
                                                       rhs=ring[:, slot, kc * 128:(kc + 1) * 128],
                                                       start=(kc == 0), stop=(kc == KC - 1))
                                return ins
                            PE(f, [('ring', slot)] + [('b1', g * TPC + i) for i in range(TPC)], PK(bank))
                            ACT(lambda e, g=g, bank=bank, vb=vb: e.activation(
                                out=vv[:, g * TPC:(g + 1) * TPC, vb * 128:(vb + 1) * 128],
                                in_=ps[bank][:, 0:CW].rearrange("p (a b) -> p a b", b=128), func=AF.Identity),
                                PK(bank), [('lav', g * TPC + i) for i in range(TPC)])
                    for hh in range(2):
                        slot = wload(win[WIN_IDX[('gg', 2 * p + hh)]])
                        for ch in range(NCH):
                            bank = gbank()
                            proj_fm(slot, big1, [('b1', ch * TPC + i) for i in range(TPC)], ch, bank)
                            ACT(lambda e, ch=ch, bank=bank, hh=hh: e.activation(
                                out=gateT[:, hh, ch * CW:(ch + 1) * CW], in_=ps[bank][:, 0:CW], func=AF.Silu),
                                PK(bank), [('gate', hh, ch)])
                    for g in range(NCH):
                        h = g % 2

                        def f(e, g=g, h=h):
                            for i in range(TPC):
                                c = g * TPC + i
                                ins = e.matmul(ps[TB[h]][:, i * 128:(i + 1) * 128], lhsT=kT[:, c * P:(c + 1) * P],
                                               rhs=identb[:], start=True, stop=True)
                            return ins
                        PE(f, [('kT', g)], PK(TB[h]))
                        ACT(lambda e, g=g, h=h: e.activation(
                            out=ktok[:, g * TPC:(g + 1) * TPC, :],
                            in_=ps[TB[h]][:, 0:CW].rearrange("p (a b) -> p a b", b=128), func=AF.Identity),
                            PK(TB[h]), [('ktok', g)])
                    for c in range(NT):
                        g = c // TPC
                        cs = slice(c * P, (c + 1) * P)
                        osl = c % TPC
                        for hh in range(2):
                            rr = slice(64 * hh, 64 * hh + 64)
                            sl = (c % 2) * 2 + hh
                            ab = 2 + hh
                            PE(lambda e, rr=rr, sl=sl, cs=cs, ab=ab: e.matmul(
                                ps[ab][:, sl * 128:(sl + 1) * 128], lhsT=kT[rr, cs], rhs=qT[rr, cs], start=True,
                                stop=True), [('kT', g), ('qT', g)], PK(ab))
                            DVE(lambda e, sl=sl, ab=ab: e.tensor_tensor(out=ATm[:, sl, :], in0=ps[ab][:, sl * 128:(sl + 1) * 128],
                                                                 in1=Ub[:], op=ALU.mult),
                                PK(ab), [('ATm', sl)])
                        for hh in range(2):
                            rr = slice(64 * hh, 64 * hh + 64)
                            sl = (c % 2) * 2 + hh
                            sb_ = hh
                            PE(lambda e, sl=sl, c=c, hh=hh, sb_=sb_: e.matmul(
                                ps[sb_][:, sl * 128:(sl + 1) * 128], lhsT=ktok[:, c, :],
                                rhs=vv[:, c, hh * 128:(hh + 1) * 128], start=True, stop=True),
                                [('ktok', g), ('lav', c)], PK(sb_))
                        for hh in range(2):
                            rr = slice(64 * hh, 64 * hh + 64)
                            sl = (c % 2) * 2 + hh
                            ob = 4 + hh

                            def f(e, sl=sl, c=c, hh=hh, rr=rr, cs=cs, ob=ob, osl=osl):
                                ins = e.matmul(ps[ob][:, osl * 128:(osl + 1) * 128], lhsT=vv[:, c, hh * 128:(hh + 1) * 128],
                                               rhs=ATm[:, sl, :], start=True, stop=(c == 0))
                                if c > 0:
                                    ins = e.matmul(ps[ob][:, osl * 128:(osl + 1) * 128], lhsT=Sbf[rr, (c - 1) % 2, :],
                                                   rhs=qT[rr, cs], start=False, stop=True)
                                return ins
                            PE(f, [('ATm', sl), ('lav', c), ('qT', g)] + ([('Sbf', hh, (c - 1) % 2)] if c > 0 else []),
                               PK(ob))
                        for hh in range(2):
                            rr = slice(64 * hh, 64 * hh + 64)
                            sl = (c % 2) * 2 + hh
                            sb_ = hh
                            if c == 0:
                                DVE(lambda e, rr=rr, sl=sl, sb_=sb_: e.tensor_copy(out=Tst[rr, :], in_=ps[sb_][rr, sl * 128:(sl + 1) * 128]),
                                    PK(sb_), [('Tst', hh)])
                            else:
                                DVE(lambda e, rr=rr, sl=sl, c=c, sb_=sb_: e.scalar_tensor_tensor(
                                    out=Tst[rr, :], in0=Tst[rr, :], scalar=elast[rr, c - 1:c],
                                    in1=ps[sb_][rr, sl * 128:(sl + 1) * 128], op0=ALU.mult, op1=ALU.add),
                                    PK(sb_) + [('Tst', hh), ('elast',)], [('Tst', hh)])
                            if c < NT - 1:
                                ACT(lambda e, rr=rr, c=c: e.activation(out=Sbf[rr, c % 2, :], in_=Tst[rr, :],
                                                                       func=AF.Identity, scale=elast[rr, c:c + 1]),
                                    [('Tst', hh), ('elast',)], [('Sbf', hh, c % 2)])
                        if c % TPC == TPC - 1:
                            for hh in range(2):
                                epilogue(4 + hh, None, gateT[:, hh, g * CW:(g + 1) * CW], [('gate', hh, g)],
                                         2 * p + hh, g * CW, CW, tmpG, hh)
                T.barrier()

            if stop == 'GLA':
                T.finish()
                return nc
            with ExitStack() as ar:
                fv = sb("fv", [P, NT, 512], BF16, ar)
                fqT = sb("fqT", [P, S], BF16, ar)
                fkT = sb("fkT", [P, S], BF16, ar)
                fgT = sb("fgT", [P, S], BF16, ar)
                lf = sb("lf", [P, NT * 4], F32, ar)
                ctok = sb("ctok", [P, NT * 4], F32, ar)
                cref = sb("cref", [P, NT * 4], F32, ar)
                biasT = sb("biasT", [P, NT, NT], F32, ar)
                PT = sb("PT", [P, 8, P], BF16, ar)
                ltmp = sb("ltmp", [P, 2, 64], F32, ar)
                tmpF = {'sq': [sb(f"sqf{i}", [P, 512], BF16, ar) for i in range(2)],
                        'rs': [sb(f"rsf{i}", [P, 512], F32, ar) for i in range(2)],
                        'rden': [sb(f"rdf{i}", [P, 512], F32, ar) for i in range(2)],
                        'on': [sb(f"onf{i}", [P, 512], F32, ar) for i in range(2)],
                        'lsa': ltmp[:, 0, :], 'lsl': ltmp[:, 1, :]}
                slot = wload(win[WIN_IDX[('small', 0)]])

                def f(e, slot=slot):
                    for t in range(NT):
                        for kc in range(KC):
                            e.matmul(ps[0][:, t * 4:(t + 1) * 4], lhsT=big1[:, kc, t * P:(t + 1) * P],
                                     rhs=ring[:, slot, kc * 128 + 32:kc * 128 + 36], start=(kc == 0), stop=False)
                        ins = e.matmul(ps[0][:, t * 4:(t + 1) * 4], lhsT=onesf[0:1, :], rhs=rowps[0:1, 512:516],
                                       start=False, stop=True)
                    return ins
                PE(f, [('ring', slot)] + [('b1', t) for t in range(NT)], PK(0))
                logsig(ps[0][:, 0:NT * 4], PK(0), lf[:], [('lf',)], NT * 4, tmpF)
                lf3 = lf[:].rearrange("p (b h) -> p b h", h=4)

                def f(e):
                    ins = e.matmul(ps[1][:, 0:NT * 4], lhsT=Uf, rhs=lf[:], start=True, stop=(NT == 1))
                    for b in range(NT - 1):
                        nb = NT - 1 - b
                        ins = e.matmul(ps[1][:, (b + 1) * 4:NT * 4].rearrange("p (b h) -> p b h", h=4), lhsT=onesf,
                                       rhs=lf3[:, b:b + 1, :].to_broadcast([P, nb, 4]),
                                       start=False, stop=(b == NT - 2))
                    return ins
                PE(f, [('lf',)], PK(1))
                ACT(lambda e: e.activation(out=ctok[:], in_=ps[1][:, 0:NT * 4], func=AF.Identity), PK(1), [('ctok',)])
                DVE(lambda e: e.memset(cref[:], 0.0), [], [('cref',)])
                if NT > 1:
                    def f(e):
                        for b in range(NT - 1):
                            nb = NT - 1 - b
                            ins = e.matmul(ps[0][:, (b + 1) * 4:NT * 4].rearrange("p (b h) -> p b h", h=4), lhsT=onesf,
                                           rhs=lf3[:, b:b + 1, :].to_broadcast([P, nb, 4]),
                                           start=(b == 0), stop=(b == NT - 2))
                        return ins
                    PE(f, [('lf',)], PK(0))
                    ACT(lambda e: e.activation(out=cref[:, 4:NT * 4], in_=ps[0][:, 4:NT * 4], func=AF.Identity),
                        PK(0) + [('cref',)], [('cref',)])
                for h in range(4):
                    slot = wload(win[WIN_IDX[('fv', h)]])
                    for g in range(NCH):
                        bank = gbank()

                        def f(e, g=g, bank=bank, slot=slot):
                            for i in range(TPC):
                                t = g * TPC + i
                                for kc in range(KC):
                                    ins = e.matmul(ps[bank][:, i * 128:(i + 1) * 128],
                                                   lhsT=big1[:, kc, t * P:(t + 1) * P],
                                                   rhs=ring[:, slot, kc * 128:(kc + 1) * 128],
                                                   start=(kc == 0), stop=(kc == KC - 1))
                            return ins
                        PE(f, [('ring', slot)] + [('b1', g * TPC + i) for i in range(TPC)], PK(bank))
                        ACT(lambda e, g=g, bank=bank, h=h: e.activation(
                            out=fv[:, g * TPC:(g + 1) * TPC, h * 128:(h + 1) * 128],
                            in_=ps[bank][:, 0:CW].rearrange("p (a b) -> p a b", b=128), func=AF.Identity),
                            PK(bank), [('fv', h)])
                ctok3 = ctok[:].rearrange("p (b h) -> p b h", h=4)
                for h in range(4):
                    for (nm, dst, kd, gi) in (('fq', fqT, 'fqT', 0), ('fk', fkT, 'fkT', 1)):
                        slot = wload(win[WIN_IDX[(nm, h)]])
                        for ch in range(NCH):
                            bank = gbank()
                            proj_fm(slot, big1, [('b1', ch * TPC + i) for i in range(TPC)], ch, bank)
                            qknorm(bank, pv[:, gi:gi + 1], dst[:, ch * CW:(ch + 1) * CW], [(kd, ch)], CW, tmpF, ch % 2)
                    slot = wload(win[WIN_IDX[('fg', h)]])
                    for ch in range(NCH):
                        bank = gbank()
                        proj_fm(slot, big1, [('b1', ch * TPC + i) for i in range(TPC)], ch, bank)
                        ACT(lambda e, ch=ch, bank=bank: e.activation(out=fgT[:, ch * CW:(ch + 1) * CW],
                                                                    in_=ps[bank][:, 0:CW], func=AF.Sigmoid),
                            PK(bank), [('fg', ch)])
                    for ib in range(NT):
                        DVE(lambda e, ib=ib, h=h: e.tensor_scalar(
                            out=biasT[:, ib, 0:ib + 1], in0=ctok3[:, 0:ib + 1, h], scalar1=-1.0,
                            scalar2=cref[:, ib * 4 + h:ib * 4 + h + 1], op0=ALU.mult, op1=ALU.add),
                            [('ctok',), ('cref',)], [('bias', ib)])
                    steps = [(ib, jb) for ib in range(NT) for jb in range(ib + 1)]
                    LA = 3

                    def emit_st(k):
                        ib, jb = steps[k]
                        sl = k % 8
                        stb = (2, 0, 7, 6)[k % 4]
                        PE(lambda e, ib=ib, jb=jb, stb=stb: e.matmul(
                            ps[stb][:, 0:128], lhsT=fkT[:, jb * P:(jb + 1) * P],
                            rhs=fqT[:, ib * P:(ib + 1) * P], start=True, stop=True),
                            [('fkT', jb // TPC), ('fqT', ib // TPC)], PK(stb))
                        ACT(lambda e, ib=ib, jb=jb, sl=sl, stb=stb: e.activation(
                            out=PT[:, sl, :], in_=ps[stb][:, 0:128], func=AF.Exp,
                            bias=biasT[:, ib, jb:jb + 1], scale=128 ** -0.5),
                            PK(stb) + [('bias', ib)], [('PT', sl)])
                        if ib == jb:
                            DVE(lambda e, sl=sl: e.tensor_tensor(out=PT[:, sl, :], in0=PT[:, sl, :], in1=Ub[:],
                                                                 op=ALU.mult), [('PT', sl)], [('PT', sl)])
                    for k in range(min(LA, len(steps))):
                        emit_st(k)
                    for k, (ib, jb) in enumerate(steps):
                        if k + LA < len(steps):
                            emit_st(k + LA)
                        sl = k % 8
                        gg = ib // TPC
                        ob = 4 + gg % 2
                        db = (3, 1)[gg % 2]
                        osl = ib % TPC

                        def f(e, ib=ib, jb=jb, sl=sl, ob=ob, db=db, osl=osl, h=h):
                            e.matmul(ps[ob][:, osl * 128:(osl + 1) * 128], lhsT=fv[:, jb, h * 128:(h + 1) * 128],
                                     rhs=PT[:, sl, :], start=(jb == 0), stop=(jb == ib))
                            return e.matmul(ps[db][:, osl * 128:(osl + 1) * 128], lhsT=onesb[:], rhs=PT[:, sl, :],
                                            start=(jb == 0), stop=(jb == ib))
                        PE(f, [('PT', sl), ('fv', h)], PK(ob, osl, 1) + PK(db, osl, 1))
                        if jb == ib and ib % TPC == TPC - 1:
                            epilogue(ob, db, fgT[:, gg * CW:(gg + 1) * CW], [('fg', gg)], 8 + h, gg * CW, CW, tmpF,
                                     gg % 2)
                T.barrier()

            if stop == 'FOX':
                T.finish()
                return nc
            with ExitStack() as ar:
                mqT = sb("mqT", [P, S], BF16, ar)
                mgT = sb("mgT", [P, S], BF16, ar)
                PTm = sb("PTm", [P, 2, 2, 512], BF16, ar)
                tmpM = {'sq': [sb(f"sqm{i}", [P, 512], BF16, ar) for i in range(2)],
                        'rs': [sb(f"rsm{i}", [P, 512], F32, ar) for i in range(2)],
                        'rden': [sb(f"rdm{i}", [P, 512], F32, ar) for i in range(2)],
                        'on': [sb(f"onm{i}", [P, 512], F32, ar) for i in range(2)]}
                for h in range(4):
                    slot = wload(win[WIN_IDX[('mq', h)]])
                    for ch in range(NCH):
                        bank = gbank()
                        proj_fm(slot, big1, [('b1', ch * TPC + i) for i in range(TPC)], ch, bank)
                        qknorm(bank, pv[:, 2:3], mqT[:, ch * CW:(ch + 1) * CW], [('mqT', ch)], CW, tmpM, ch % 2)
                    slot = wload(win[WIN_IDX[('mg', h)]])
                    for ch in range(NCH):
                        bank = gbank()
                        proj_fm(slot, big1, [('b1', ch * TPC + i) for i in range(TPC)], ch, bank)
                        ACT(lambda e, ch=ch, bank=bank: e.activation(out=mgT[:, ch * CW:(ch + 1) * CW],
                                                                    in_=ps[bank][:, 0:CW], func=AF.Sigmoid),
                            PK(bank), [('mg', ch)])
                    for ch in range(NCH):
                        par = ch % 2
                        for mc in range(2):
                            PE(lambda e, mc=mc, ch=ch, h=h: e.matmul(
                                ps[2 + mc][:, 0:CW], lhsT=mkT[:, h, mc * P:(mc + 1) * P],
                                rhs=mqT[:, ch * CW:(ch + 1) * CW], start=True, stop=True),
                                [('mkT', h), ('mqT', ch)], PK(2 + mc))
                            ACT(lambda e, mc=mc, par=par: e.activation(out=PTm[:, par, mc, 0:CW], in_=ps[2 + mc][:, 0:CW],
                                                                       func=AF.Exp, scale=128 ** -0.5),
                                PK(2 + mc), [('PTm', par, mc)])
                        ob, db = 4 + par, (0, 1)[par]

                        def f(e, par=par, ob=ob, db=db, h=h):
                            for mc in range(2):
                                e.matmul(ps[ob][:, 0:CW], lhsT=mv[:, mc, h * 128:(h + 1) * 128], rhs=PTm[:, par, mc, 0:CW],
                                         start=(mc == 0), stop=(mc == 1))
                            for mc in range(2):
                                ins = e.matmul(ps[db][:, 0:CW], lhsT=onesb[:], rhs=PTm[:, par, mc, 0:CW],
                                               start=(mc == 0), stop=(mc == 1))
                            return ins
                        PE(f, [('PTm', par, 0), ('PTm', par, 1), ('mv', h)], PK(ob) + PK(db))
                        epilogue(ob, db, mgT[:, ch * CW:(ch + 1) * CW], [('mg', ch)], 12 + h, ch * CW, CW, tmpM, par)
                T.barrier()

            if stop == 'MEM':
                T.finish()
                return nc
            with ExitStack() as ar:
                hst = sb("hst", [P, TPC, D], F32, ar)
                xnb = sb("xnb2", [P, 2, D], BF16, ar)
                gbc = sb("gbc2", [P, D], F32, ar)
                sqj = sb("sqj2", [P, D], BF16, ar)
                load_gbc(2)
                for tt in range(NCH):
                    for sub in range(TPC):
                        t = tt * TPC + sub
                        T.dma('sp', ('hst', sub), lambda e, t=t, sub=sub: e.dma_start(
                            out=hst[:, sub, :], in_=x[r0 + t * P:r0 + (t + 1) * P, :]), writes=[('hst', sub)])
                    for j in range(16):
                        slot = wload(wout[j])
                        bank = gbank((0, 1, 2, 3))

                        def f(e, slot=slot, bank=bank, tt=tt):
                            for sub in range(TPC):
                                t = tt * TPC + sub
                                for hc in range(16):
                                    ins = e.matmul(ps[bank][:, sub * 128:(sub + 1) * 128],
                                                   lhsT=ogT[:, hc, t * P:(t + 1) * P],
                                                   rhs=ring[:, slot, hc * 128:(hc + 1) * 128],
                                                   start=(hc == 0), stop=(hc == 15))
                            return ins
                        PE(f, [('ring', slot)] + [('og', hc, tt) for hc in range(16)], PK(bank))
                        DVE(lambda e, bank=bank, j=j: e.tensor_tensor(
                            out=hst[:, :, j * 128:(j + 1) * 128], in0=ps[bank][:, 0:CW].rearrange("p (a b) -> p a b", b=128),
                            in1=hst[:, :, j * 128:(j + 1) * 128], op=ALU.add),
                            PK(bank) + [('hst', sub) for sub in range(TPC)], [('hst', sub) for sub in range(TPC)])
                    pend_s2 = None
                    for sub in range(TPC):
                        t = tt * TPC + sub
                        s2 = norm_rows(hst[:, sub, :], [('hst', sub)], gbc, xnb, sub % 2, big1, t * P, sub % 2)
                        T.dma('sp', ('hout', sub), lambda e, t=t, sub=sub: e.dma_start(
                            out=out[r0 + t * P:r0 + (t + 1) * P, :], in_=hst[:, sub, :]),
                            reads=[('hst', sub)], writes=[('outrow', s, t)], is_out=True)
                        if pend_s2 is not None:
                            pend_s2()
                        pend_s2 = s2
                    pend_s2()
                T.barrier()

            if stop == 'B1':
                T.finish()
                return nc
            with ExitStack() as ar:
                hres = sb("hres", [P, 2, TPC, 512], F32, ar)
                rtmp = sb("rtmp", [P, 2, 512], F32, ar)
                for tt in range(NCH):
                    c0 = tt * CW
                    for f_ in range(64):
                        slot = wload(wup[f_])
                        bank = gbank((4, 5, 6))
                        proj_fm(slot, big1, [('b1', tt * TPC + i) for i in range(TPC)], tt, bank)
                        par = f_ % 2
                        ACT(lambda e, bank=bank, par=par: e.activation(out=rtmp[:, par, 0:CW], in_=ps[bank][:, 0:CW],
                                                                       func=AF.Relu), PK(bank), [('rtmp', par)])
                        DVE(lambda e, par=par, f_=f_: e.tensor_tensor(out=uT[:, f_, 0:CW], in0=rtmp[:, par, 0:CW],
                                                                       in1=rtmp[:, par, 0:CW], op=ALU.mult),
                            [('rtmp', par)], [('uT', f_)])
                    for j in range(4):
                        hp = j % 2
                        for sub in range(TPC):
                            t = tt * TPC + sub
                            T.dma('sp', ('hres', hp, sub), lambda e, t=t, sub=sub, hp=hp, j=j: e.dma_start(
                                out=hres[:, hp, sub, :], in_=out[r0 + t * P:r0 + (t + 1) * P, j * 512:(j + 1) * 512]),
                                reads=[('outrow', s, t)], writes=[('hres', hp, sub)])
                        for fg in range(16):
                            slot = wload(wdn[j * 16 + fg])

                            def f(e, slot=slot, fg=fg):
                                for fi in range(4):
                                    ff = fg * 4 + fi
                                    for sub in range(TPC):
                                        ins = e.matmul(ps[sub][:, :], lhsT=uT[:, ff, sub * P:(sub + 1) * P],
                                                       rhs=ring[:, slot, fi * 512:(fi + 1) * 512],
                                                       start=(ff == 0), stop=(ff == 63))
                                return ins
                            PE(f, [('ring', slot)] + [('uT', fg * 4 + fi) for fi in range(4)],
                               [k for sub in range(TPC) for k in PK(sub)])
                        for sub in range(TPC):
                            t = tt * TPC + sub
                            DVE(lambda e, sub=sub, hp=hp: e.tensor_tensor(out=hres[:, hp, sub, :], in0=ps[sub][:, :],
                                                                         in1=hres[:, hp, sub, :], op=ALU.add),
                                PK(sub) + [('hres', hp, sub)], [('hres', hp, sub)])
                            T.dma('sp', ('yout', hp, sub), lambda e, t=t, sub=sub, hp=hp, j=j: e.dma_start(
                                out=out[r0 + t * P:r0 + (t + 1) * P, j * 512:(j + 1) * 512], in_=hres[:, hp, sub, :]),
                                reads=[('hres', hp, sub)], writes=[('outfin', s, t, j)], is_out=True)
                T.barrier()
        T.finish()
    return nc


def _host_layouts(inp):
    f32 = np.float32
    w_in = np.asarray(inp["w_in"], f32)
    blocks = []
    for (nm, i) in WIN_BLOCKS:
        if nm == 'small':
            blk = np.zeros((D, 128), f32)
            blk[:, 0:16] = w_in[:, 3072:3088]
            blk[:, 32:36] = w_in[:, 5136:5140]
        else:
            c = win_cols(nm, i)
            blk = w_in[:, c[0]:c[-1] + 1]
        blocks.append(blk.reshape(KC, P, 128).transpose(1, 0, 2).reshape(P, 2048))
    win = np.ascontiguousarray(np.stack(blocks))

    def colblocks(w, nb):
        K = w.shape[0] // P
        return np.ascontiguousarray(
            w.reshape(K, P, nb, 128).transpose(2, 1, 0, 3).reshape(nb, P, K * 128))

    wmem = colblocks(np.asarray(inp["w_mem_kv"], f32), 8)
    wout = colblocks(np.asarray(inp["w_out"], f32), 16)
    wup = colblocks(np.asarray(inp["w_up"], f32), 64)
    wd = np.asarray(inp["w_down"], f32)
    wdn = np.ascontiguousarray(
        wd.reshape(16, 4, P, 4, 512).transpose(3, 0, 2, 1, 4).reshape(64, P, 2048))
    consts = np.zeros((P, 384), f32)
    consts[:, 0:128] = np.eye(P, dtype=f32)
    consts[:, 128:256] = np.triu(np.ones((P, P), f32))
    consts[:, 256:384] = 1.0
    pvec = np.zeros((P, 20), f32)
    pvec[:, 0] = inp["fox_q_norm_g"]
    pvec[:, 1] = inp["fox_k_norm_g"]
    pvec[:, 2] = inp["mem_q_norm_g"]
    pvec[:, 3] = inp["mem_k_norm_g"]
    pvec[:, 4:20] = np.asarray(inp["out_norm_g"], f32).reshape(16, P).T
    gvecs = np.stack([inp["attn_norm_g"], inp["mem_norm_g"], inp["mlp_norm_g"]]).astype(f32)
    rowp = np.concatenate([inp["gla_a_b"], inp["fox_f_b"]]).astype(f32)[None, :]
    return dict(consts=consts, pvec=pvec, gvecs=np.ascontiguousarray(gvecs),
                w2=np.ascontiguousarray(inp["gla_a_w2"], dtype=f32), rowp=np.ascontiguousarray(rowp),
                win=win, wmem=wmem, wout=wout, wup=wup, wdn=wdn)


def run(inputs, n_cores, trace=False):
    x = np.asarray(inputs["x"], np.float32)
    mem = np.asarray(inputs["mem"], np.float32)
    B, S, _ = x.shape
    nseq = B // n_cores
    shared = _host_layouts(inputs)
    import os
    nc = build(nseq, S, os.environ.get('KSTOP'))
    in_maps = []
    for c in range(n_cores):
        m = dict(shared)
        m["x"] = np.ascontiguousarray(x[c * nseq:(c + 1) * nseq].reshape(nseq * S, D))
        m["mem"] = np.ascontiguousarray(mem[c * nseq:(c + 1) * nseq].reshape(nseq * 256, D))
        in_maps.append(m)
    res = run_bass_kernel_spmd(nc, in_maps, core_ids=list(range(n_cores)), trace=trace)
    outs = [r["out"].reshape(nseq, S, D) for r in res.results]
    return np.concatenate(outs, axis=0).astype(np.float32), res


def kernel(**inputs):
    y, _ = run(inputs, 8)
    return y
```

```python
import numpy as np
from contextlib import ExitStack
import concourse.bass as bass
import concourse.mybir as mybir
from concourse.bass_utils import run_bass_kernel_spmd

F32 = mybir.dt.float32
BF16 = mybir.dt.bfloat16
AF = mybir.ActivationFunctionType
ALU = mybir.AluOpType
P = 128
D = 2048
KC = 16
EPS = 1e-6
NRING = 3


class Tracker:
    def __init__(self, nc, st):
        self.nc = nc
        self.st = st
        self.eng = {'pe': nc.tensor, 'act': nc.scalar, 'dve': nc.vector, 'pool': nc.gpsimd, 'sp': nc.sync}
        self.sem = {e: st.enter_context(nc.semaphore('s_' + e)) for e in ('pe', 'act', 'dve')}
        self.cnt = {e: 0 for e in self.sem}
        self.seen = {e: {} for e in self.eng}
        self.lw = {}
        self.rd = {}
        self.dsem = {}
        self.dcnt = {}
        self.out_toks = {}

    def _wait(self, eng, toks):
        need = {}
        seen = self.seen[eng]
        for (key, sem, val) in toks:
            if seen.get(key, 0) >= val:
                continue
            if key not in need or need[key][1] < val:
                need[key] = (sem, val)
        for key, (sem, val) in need.items():
            self.eng[eng].wait_ge(sem, val)
            seen[key] = val

    def _collect(self, eng, reads, writes):
        toks = []
        for r in reads:
            t = self.lw.get(r)
            if t is not None and not (t[3] == eng and eng == 'pe'):
                toks.append(t[:3])
        for w in writes:
            t = self.lw.get(w)
            if t is not None and (t[3] != eng or eng is None):
                toks.append(t[:3])
            for t in self.rd.get(w, ()):
                if t[3] != eng or eng is None:
                    toks.append(t[:3])
        return toks

    def _commit(self, tok, reads, writes):
        for w in writes:
            self.lw[w] = tok
            self.rd[w] = []
        for r in reads:
            self.rd.setdefault(r, []).append(tok)

    def op(self, eng, fn, reads=(), writes=()):
        psr = [r for r in reads if r[0] == 'ps']
        if psr:
            reads = [r for r in reads if r[0] != 'ps']
            writes = list(writes) + [r for r in psr if r not in writes]
        self._wait(eng, self._collect(eng, reads, writes))
        ins = fn(self.eng[eng])
        self.cnt[eng] += 1
        ins.then_inc(self.sem[eng], 1)
        self._commit((eng, self.sem[eng], self.cnt[eng], eng), reads, writes)

    def dma(self, q, key, fn, reads=(), writes=(), is_out=False):
        self._wait(q, self._collect(None, reads, writes))
        if key not in self.dsem:
            self.dsem[key] = self.st.enter_context(self.nc.semaphore('d_' + '_'.join(str(k) for k in key)))
            self.dcnt[key] = 0
        ins = fn(self.eng[q])
        self.dcnt[key] += 16
        ins.then_inc(self.dsem[key], 16)
        tok = (('d',) + tuple(key), self.dsem[key], self.dcnt[key], None)
        self._commit(tok, reads, writes)
        if is_out:
            self.out_toks[key] = tok

    def barrier(self):
        toks = [(e, self.sem[e], self.cnt[e]) for e in self.sem if self.cnt[e]]
        toks += [(('d',) + tuple(k), self.dsem[k], self.dcnt[k]) for k in self.dsem if k[0] != 'ring']
        for e in ('pe', 'act', 'dve', 'sp'):
            self._wait(e, toks)

    def finish(self):
        self._wait('sp', [t[:3] for t in self.out_toks.values()])


def win_cols(name, i):
    base = {'gq': 0, 'gk': 512, 'gv': 1024, 'gg': 2048, 'fq': 3088, 'fk': 3600, 'fv': 4112,
            'fg': 4624, 'mq': 5140, 'mg': 5652}[name]
    return list(range(base + 128 * i, base + 128 * i + 128))


def win_block_list():
    blocks = [('small', 0)]
    for nm, n in (('gq', 4), ('gk', 4), ('gv', 8), ('gg', 8), ('fq', 4), ('fk', 4), ('fv', 4), ('fg', 4),
                  ('mq', 4), ('mg', 4)):
        blocks += [(nm, i) for i in range(n)]
    return blocks


WIN_BLOCKS = win_block_list()
WIN_IDX = {b: i for i, b in enumerate(WIN_BLOCKS)}


def build(NSEQ, S, stop=None):
    NT = S // P
    CW = min(512, S)
    NCH = S // CW
    TPC = CW // P
    NM = 256
    nc = bass.Bass("TRN2", target_bir_lowering=False)

    def din(name, shape):
        return nc.dram_tensor(name, list(shape), F32, kind="ExternalInput").ap()

    x = din("x", [NSEQ * S, D])
    mem = din("mem", [NSEQ * NM, D])
    consts = din("consts", [P, 384])
    pvec = din("pvec", [P, 20])
    gvecs = din("gvecs", [3, D])
    w2d = din("w2", [16, 512])
    rowp = din("rowp", [1, 516])
    win = din("win", [49, P, 2048])
    wmem = din("wmem", [8, P, 2048])
    wout = din("wout", [16, P, 2048])
    wup = din("wup", [64, P, 2048])
    wdn = din("wdn", [64, P, 2048])
    out = nc.dram_tensor("out", [NSEQ * S, D], F32, kind="ExternalOutput").ap()

    with ExitStack() as st:
        T = Tracker(nc, st)

        uid = [0]

        def sb(name, shape, dt, stack=st):
            uid[0] += 1
            return stack.enter_context(nc.sbuf_tensor(f"{name}_{uid[0]}", list(shape), dt))

        ps = [st.enter_context(nc.psum_tensor(f"ps{i}", [P, 512], F32)) for i in range(8)]
        TB = (7, 6)

        def PK(b, lo=0, n=4):
            return [('ps', b)]

        def ACT(fn, r=(), w=()):
            T.op('act', fn, r, w)

        def DVE(fn, r=(), w=()):
            T.op('dve', fn, r, w)

        def PE(fn, r=(), w=()):
            T.op('pe', fn, r, w)

        big1 = sb("big1", [P, KC, S], BF16)
        big2 = sb("big2", [P, max(16 * S, 64 * CW)], BF16)
        ogT = big2[:, 0:16 * S].rearrange("p (h s) -> p h s", h=16)
        uT = big2[:, 0:64 * CW].rearrange("p (f t) -> p f t", f=64)
        ring = sb("ring", [P, NRING, 2048], BF16)
        cf = sb("cf", [P, 384], F32)
        identb = sb("identb", [P, P], BF16)
        Ub = sb("Ub", [P, P], BF16)
        onesb = sb("onesb", [P, P], BF16)
        onesDb = sb("onesDb", [P, P], BF16)
        U16 = sb("U16", [P, P], F32)
        pv = sb("pv", [P, 20], F32)
        w2f = sb("w2f", [16, 512], F32)
        w2b = sb("w2b", [16, 512], BF16)
        rowps = sb("rowps", [1, 516], F32)
        epsc = sb("epsc", [P, 1], F32)
        mkT = sb("mkT", [P, 4, NM], BF16)
        mv = sb("mv", [P, 2, 512], BF16)
        ssb = sb("ssb", [P, 8], F32)
        rsb = sb("rsb", [P, 8], F32)
        identf = cf[:, 0:128]
        Uf = cf[:, 128:256]
        onesf = cf[:, 256:384]

        T.dma('sp', ('c', 0), lambda e: e.dma_start(out=cf[:], in_=consts), writes=[('cf',)])
        T.dma('sp', ('c', 1), lambda e: e.dma_start(out=pv[:], in_=pvec), writes=[('pv',)])
        T.dma('sp', ('c', 2), lambda e: e.dma_start(out=w2f[:], in_=w2d), writes=[('w2f',)])
        T.dma('sp', ('c', 3), lambda e: e.dma_start(out=rowps[:], in_=rowp), writes=[('rowps',)])
        DVE(lambda e: e.tensor_copy(out=identb[:], in_=identf), [('cf',)], [('k1',)])
        DVE(lambda e: e.tensor_copy(out=Ub[:], in_=Uf), [('cf',)], [('k2',)])
        DVE(lambda e: e.tensor_copy(out=onesb[:], in_=onesf), [('cf',)], [('k3',)])
        DVE(lambda e: e.tensor_scalar(out=onesDb[:], in0=onesf, scalar1=1.0 / 128, scalar2=None, op0=ALU.mult),
            [('cf',)], [('k4',)])
        DVE(lambda e: e.tensor_scalar(out=U16[:], in0=Uf, scalar1=1.0 / 16, scalar2=None, op0=ALU.mult),
            [('cf',)], [('k5',)])
        DVE(lambda e: e.tensor_copy(out=w2b[:], in_=w2f[:]), [('w2f',)], [('k6',)])
        DVE(lambda e: e.memset(epsc[:], EPS), [], [('k7',)])
        T.barrier()
        if stop == 'INIT':
            T.finish()
            return nc

        wcount = [0]

        def wload(src):
            slot = wcount[0] % NRING
            wcount[0] += 1
            T.dma('pool', ('ring', slot), lambda e: e.dma_start(out=ring[:, slot, :], in_=src),
                  writes=[('ring', slot)])
            return slot

        rot = [0]

        def gbank(banks=(0, 1)):
            b = banks[rot[0] % len(banks)]
            rot[0] += 1
            return b

        def proj_fm(slot, srcT, ksrc, ch, bank, n=None, c0=None):
            n = CW if n is None else n
            c0 = ch * CW if c0 is None else c0

            def f(e):
                for kc in range(KC):
                    ins = e.matmul(ps[bank][:, 0:n], lhsT=ring[:, slot, kc * 128:(kc + 1) * 128],
                                   rhs=srcT[:, kc, c0:c0 + n], start=(kc == 0), stop=(kc == KC - 1))
                return ins
            PE(f, [('ring', slot)] + ksrc, PK(bank))

        def rms_cols(src, ksrc, n, tmp, par):
            sq, rs = tmp['sq'][par], tmp['rs'][par]
            ACT(lambda e: e.activation(out=sq[:, 0:n], in_=src, func=AF.Square), ksrc, [('sq', par)])
            PE(lambda e: e.matmul(ps[6][:, 0:n], lhsT=onesDb[:], rhs=sq[:, 0:n], start=True, stop=True),
               [('sq', par)], PK(6))
            ACT(lambda e: e.activation(out=rs[:, 0:n], in_=ps[6][:, 0:n], func=AF.Sqrt, bias=epsc[:], scale=1.0),
                PK(6), [('rs', par)])
            DVE(lambda e: e.reciprocal(out=rs[:, 0:n], in_=rs[:, 0:n]), [('rs', par)], [('rs', par)])
            return rs

        def qknorm(bank, gain, dst, kdst, n, tmp, par):
            rs = rms_cols(ps[bank][:, 0:n], PK(bank), n, tmp, par)
            DVE(lambda e: e.scalar_tensor_tensor(out=dst, in0=ps[bank][:, 0:n], scalar=gain, in1=rs[:, 0:n],
                                                 op0=ALU.mult, op1=ALU.mult),
                PK(bank) + [('rs', par)], kdst)

        def epilogue(obank, dbank, gate, kgate, head, t0, n, tmp, par):
            if dbank is not None:
                rden, on = tmp['rden'][par], tmp['on'][par]
                DVE(lambda e: e.reciprocal(out=rden[:, 0:n], in_=ps[dbank][:, 0:n]), PK(dbank), [('rden', par)])
                DVE(lambda e: e.tensor_tensor(out=on[:, 0:n], in0=ps[obank][:, 0:n], in1=rden[:, 0:n], op=ALU.mult),
                    PK(obank) + [('rden', par)], [('on', par)])
                src, ksrc = on[:, 0:n], [('on', par)]
            else:
                src, ksrc = ps[obank][:, 0:n], PK(obank)
            rs = rms_cols(src, ksrc, n, tmp, par)
            DVE(lambda e: e.tensor_tensor(out=rs[:, 0:n], in0=rs[:, 0:n], in1=gate, op=ALU.mult),
                [('rs', par)] + kgate, [('rs', par)])
            DVE(lambda e: e.scalar_tensor_tensor(out=ogT[:, head, t0:t0 + n], in0=src,
                                                 scalar=pv[:, 4 + head:5 + head], in1=rs[:, 0:n],
                                                 op0=ALU.mult, op1=ALU.mult),
                ksrc + [('rs', par)], [('og', head, t0 // CW)])

        def logsig(zsrc, kz, dst, kdst, n, tmp, ka=('lsa',), kl=('lsl',)):
            a, l = tmp['lsa'], tmp['lsl']
            ACT(lambda e: e.activation(out=a[:, 0:n], in_=zsrc, func=AF.Abs), kz, [ka])
            ACT(lambda e: e.activation(out=a[:, 0:n], in_=a[:, 0:n], func=AF.Exp, scale=-1.0), [ka], [ka])
            ACT(lambda e: e.activation(out=l[:, 0:n], in_=a[:, 0:n], func=AF.Ln, bias=1.0, scale=1.0),
                [ka], [kl])
            DVE(lambda e: e.scalar_tensor_tensor(out=dst, in0=zsrc, scalar=0.0, in1=l[:, 0:n],
                                                 op0=ALU.min, op1=ALU.subtract),
                kz + [kl], kdst)

        def norm_rows(src, ksrc, gbc, xnb, par, dstT, tok0, cell):
            DVE(lambda e: e.memset(ssb[:, cell:cell + 1], 0.0), [], [('ss', cell)])
            ACT(lambda e: e.activation(out=sqj[:], in_=src, func=AF.Square, accum_out=ssb[:, cell:cell + 1]),
                ksrc + [('ss', cell)], [('sqj',), ('ss', cell)])
            ACT(lambda e: e.activation(out=rsb[:, cell:cell + 1], in_=ssb[:, cell:cell + 1], func=AF.Sqrt,
                                       bias=epsc[:], scale=1.0 / D), [('ss', cell)], [('rsb', cell)])
            DVE(lambda e: e.reciprocal(out=rsb[:, cell:cell + 1], in_=rsb[:, cell:cell + 1]),
                [('rsb', cell)], [('rsb', cell)])
            DVE(lambda e: e.scalar_tensor_tensor(out=xnb[:, par, :], in0=src, scalar=rsb[:, cell:cell + 1],
                                                 in1=gbc[:], op0=ALU.mult, op1=ALU.mult),
                ksrc + [('rsb', cell), ('gbc',)], [('xnb', par)])

            def stage2():
                for q in range(4):
                    h = q % 2

                    def f(e, q=q, h=h):
                        for i in range(4):
                            kc = q * 4 + i
                            ins = e.matmul(ps[TB[h]][:, i * 128:(i + 1) * 128],
                                           lhsT=xnb[:, par, kc * 128:(kc + 1) * 128],
                                           rhs=identb[:], start=True, stop=True)
                        return ins
                    PE(f, [('xnb', par)], PK(TB[h]))
                    ACT(lambda e, q=q, h=h: e.activation(
                        out=dstT[:, q * 4:(q + 1) * 4, tok0:tok0 + P],
                        in_=ps[TB[h]][:, 0:512].rearrange("p (a b) -> p a b", a=4), func=AF.Identity),
                        PK(TB[h]), [('b1', tok0 // P)])
            return stage2

        def load_gbc(idx):
            T.dma('sp', ('gbc',), lambda e: e.dma_start(out=gbc[:], in_=gvecs[idx].partition_broadcast(P)),
                  writes=[('gbc',)])

        for s in range(NSEQ):
            r0 = s * S
            with ExitStack() as ar:
                stg = sb("stg", [P, 2, D], F32, ar)
                xnb = sb("xnb", [P, 2, D], BF16, ar)
                gbc = sb("gbc", [P, D], F32, ar)
                sqj = sb("sqj", [P, D], BF16, ar)
                mnT = sb("mnT", [P, KC, NM], BF16, ar)
                tmpA = {'sq': [sb(f"sqa{i}", [P, 512], BF16, ar) for i in range(2)],
                        'rs': [sb(f"rsa{i}", [P, 512], F32, ar) for i in range(2)]}
                load_gbc(0)
                pend_s2 = None
                for t in range(NT):
                    sl = t % 2
                    T.dma('sp', ('stg', sl), lambda e, t=t, sl=sl: e.dma_start(out=stg[:, sl, :],
                                                                             in_=x[r0 + t * P:r0 + (t + 1) * P, :]),
                          writes=[('stg', sl)])
                    s2 = norm_rows(stg[:, sl, :], [('stg', sl)], gbc, xnb, sl, big1, t * P, sl)
                    if pend_s2 is not None:
                        pend_s2()
                    pend_s2 = s2
                pend_s2()
                if stop == 'A0':
                    T.barrier()
                    T.finish()
                    return nc
                load_gbc(1)
                for t in range(2):
                    sl = t % 2
                    T.dma('sp', ('stg', sl), lambda e, t=t, sl=sl: e.dma_start(
                        out=stg[:, sl, :], in_=mem[s * NM + t * P:s * NM + (t + 1) * P, :]), writes=[('stg', sl)])
                    DVE(lambda e, sl=sl: e.memset(ssb[:, sl:sl + 1], 0.0), [], [('ss', sl)])
                    src = stg[:, sl, :]
                    ACT(lambda e, src=src, sl=sl: e.activation(out=sqj[:], in_=src, func=AF.Square,
                                                               accum_out=ssb[:, sl:sl + 1]),
                        [('stg', sl), ('ss', sl)], [('sqj',), ('ss', sl)])
                    ACT(lambda e, sl=sl: e.activation(out=rsb[:, sl:sl + 1], in_=ssb[:, sl:sl + 1], func=AF.Sqrt,
                                                      bias=epsc[:], scale=1.0 / D), [('ss', sl)], [('rsb', sl)])
                    DVE(lambda e, sl=sl: e.reciprocal(out=rsb[:, sl:sl + 1], in_=rsb[:, sl:sl + 1]),
                        [('rsb', sl)], [('rsb', sl)])
                    DVE(lambda e, src=src, sl=sl: e.scalar_tensor_tensor(
                        out=xnb[:, sl, :], in0=src, scalar=rsb[:, sl:sl + 1], in1=gbc[:], op0=ALU.mult,
                        op1=ALU.mult), [('stg', sl), ('rsb', sl), ('gbc',)], [('xnb', sl)])
                    for q in range(4):
                        h = q % 2

                        def f(e, q=q, h=h, sl=sl):
                            for i in range(4):
                                kc = q * 4 + i
                                ins = e.matmul(ps[TB[h]][:, i * 128:(i + 1) * 128], lhsT=xnb[:, sl, kc * 128:(kc + 1) * 128],
                                               rhs=identb[:], start=True, stop=True)
                            return ins
                        PE(f, [('xnb', sl)], PK(TB[h]))
                        ACT(lambda e, q=q, h=h, t=t: e.activation(
                            out=mnT[:, q * 4:(q + 1) * 4, t * P:(t + 1) * P],
                            in_=ps[TB[h]][:, 0:512].rearrange("p (a b) -> p a b", a=4), func=AF.Identity),
                            PK(TB[h]), [('mn',)])
                for h in range(4):
                    slot = wload(wmem[h])
                    bank = gbank()
                    proj_fm(slot, mnT, [('mn',)], 0, bank, n=NM, c0=0)
                    qknorm(bank, pv[:, 3:4], mkT[:, h, :], [('mkT', h)], NM, tmpA, h % 2)
                for h in range(4):
                    slot = wload(wmem[4 + h])
                    bank = gbank()

                    def f(e, slot=slot, bank=bank):
                        for mc in range(2):
                            for kc in range(KC):
                                ins = e.matmul(ps[bank][:, mc * 128:(mc + 1) * 128],
                                               lhsT=mnT[:, kc, mc * P:(mc + 1) * P],
                                               rhs=ring[:, slot, kc * 128:(kc + 1) * 128],
                                               start=(kc == 0), stop=(kc == KC - 1))
                        return ins
                    PE(f, [('ring', slot), ('mn',)], PK(bank, 0, 2))
                    ACT(lambda e, bank=bank, h=h: e.activation(
                        out=mv[:, :, h * 128:(h + 1) * 128],
                        in_=ps[bank][:, 0:256].rearrange("p (a b) -> p a b", a=2), func=AF.Identity),
                        PK(bank, 0, 2), [('mv', h)])
                T.barrier()

            if stop == 'A1':
                T.finish()
                return nc
            with ExitStack() as ar:
                gaT = sb("gaT", [32, S], BF16, ar)
                lav = sb("lav", [P, NT * 256], BF16, ar)
                laf = lav[:].bitcast(F32)
                la_p = laf.rearrange("p (c f) -> p c f", f=128)
                vv = lav[:].rearrange("p (c f) -> p c f", f=256)
                bT = sb("bT", [P, S], F32, ar)
                qT = sb("qT", [P, S], BF16, ar)
                kT = sb("kT", [P, S], BF16, ar)
                ktok = sb("ktok", [P, NT, P], BF16, ar)
                gateT = sb("gateT", [P, 2, S], BF16, ar)
                elast = sb("elast", [P, NT], F32, ar)
                Tst = sb("Tst", [P, P], F32, ar)
                Sbf = sb("Sbf", [P, 2, P], BF16, ar)
                ATm = sb("ATm", [P, 4, P], BF16, ar)
                Etmp = sb("Etmp", [P, 2, 512], F32, ar)
                tmpG = {'sq': [sb(f"sqg{i}", [P, 512], BF16, ar) for i in range(2)],
                        'rs': [sb(f"rsg{i}", [P, 512], F32, ar) for i in range(2)],
                        'lsa': Etmp[:, 0, :], 'lsl': Etmp[:, 1, :]}
                slot = wload(win[WIN_IDX[('small', 0)]])
                for ch in range(NCH):
                    bank = gbank()

                    def f(e, slot=slot, bank=bank, ch=ch):
                        for kc in range(KC):
                            ins = e.matmul(ps[bank][0:32, 0:CW], lhsT=ring[:, slot, kc * 128:kc * 128 + 32],
                                           rhs=big1[:, kc, ch * CW:(ch + 1) * CW], start=(kc == 0),
                                           stop=(kc == KC - 1))
                        return ins
                    PE(f, [('ring', slot)] + [('b1', ch * TPC + i) for i in range(TPC)], PK(bank))
                    ACT(lambda e, bank=bank, ch=ch: e.activation(out=gaT[:, ch * CW:(ch + 1) * CW],
                                                                 in_=ps[bank][0:32, 0:CW], func=AF.Identity),
                        PK(bank), [('gaT', ch)])
                for p in range(4):
                    for g in range(NCH):
                        bank = gbank()

                        def f(e, g=g, bank=bank, p=p):
                            for i in range(TPC):
                                t = g * TPC + i
                                e.matmul(ps[bank][:, i * 128:(i + 1) * 128], lhsT=gaT[0:16, t * P:(t + 1) * P],
                                         rhs=w2b[0:16, p * 128:(p + 1) * 128], start=True, stop=False)
                                ins = e.matmul(ps[bank][:, i * 128:(i + 1) * 128], lhsT=onesf[0:1, :],
                                               rhs=rowps[0:1, p * 128:(p + 1) * 128], start=False, stop=True)
                            return ins
                        PE(f, [('gaT', g)], PK(bank))
                        logsig(ps[bank][:, 0:CW], PK(bank), laf[:, g * CW:(g + 1) * CW],
                               [('lav', g * TPC + i) for i in range(TPC)], CW, tmpG, ('Etmp', 0), ('Etmp', 1))
                    for g in range(NCH):
                        bank = gbank()

                        def f(e, g=g, bank=bank):
                            for i in range(TPC):
                                c = g * TPC + i
                                ins = e.matmul(ps[bank][:, i * 128:(i + 1) * 128], lhsT=la_p[:, c, :], rhs=U16[:],
                                               start=True, stop=True)
                            return ins
                        PE(f, [('lav', g * TPC + i) for i in range(TPC)], PK(bank))
                        ACT(lambda e, g=g, bank=bank: e.activation(out=bT[:, g * CW:(g + 1) * CW],
                                                                   in_=ps[bank][:, 0:CW], func=AF.Identity),
                            PK(bank), [('bT', g)])
                    ACT(lambda e: e.activation(out=elast[:].rearrange("p (c o) -> p c o", o=1),
                                               in_=bT[:].rearrange("p (c t) -> p c t", t=128)[:, :, 127:128],
                                               func=AF.Exp), [('bT', g) for g in range(NCH)], [('elast',)])
                    for (nm, dst, kd, sgn, scl) in (('gq', qT, 'qT', 1.0, 0.125), ('gk', kT, 'kT', -1.0, 1.0)):
                        slot = wload(win[WIN_IDX[(nm, p)]])
                        for ch in range(NCH):
                            bank = gbank()
                            proj_fm(slot, big1, [('b1', ch * TPC + i) for i in range(TPC)], ch, bank)
                            par = ch % 2
                            ACT(lambda e, ch=ch, par=par, sgn=sgn: e.activation(
                                out=Etmp[:, par, 0:CW], in_=bT[:, ch * CW:(ch + 1) * CW], func=AF.Exp, scale=sgn),
                                [('bT', ch)], [('Etmp', par)])
                            DVE(lambda e, ch=ch, par=par, bank=bank, dst=dst, scl=scl: e.scalar_tensor_tensor(
                                out=dst[:, ch * CW:(ch + 1) * CW], in0=ps[bank][:, 0:CW], scalar=scl,
                                in1=Etmp[:, par, 0:CW], op0=ALU.mult, op1=ALU.mult),
                                PK(bank) + [('Etmp', par)], [(kd, ch)])
                    for vb in range(2):
                        slot = wload(win[WIN_IDX[('gv', 2 * p + vb)]])
                        for g in range(NCH):
                            bank = gbank()

                            def f(e, g=g, bank=bank, slot=slot):
                                for i in range(TPC):
                                    t = g * TPC + i
                                    for kc in range(KC):
                                        ins = e.matmul(ps[bank][:, i * 128:(i + 1) * 128],
                                                       lhsT=big1[:, kc, t * P:(t + 1) * P],
                                                       rhs=ring[:, slot, kc * 128:(kc + 1) * 128],
                                                       start=(kc == 0), stop=(kc == KC - 1))
                                return ins
                            PE(f, [('ring', slot)] + [('b1', g * TPC + i) for i in range(TPC)], PK(bank))
                            ACT(lambda e, g=g, bank=bank, vb=vb: e.activation(
                                out=vv[:, g * TPC:(g + 1) * TPC, vb * 128:(vb + 1) * 128],
                                in_=ps[bank][:, 0:CW].rearrange("p (a b) -> p a b", b=128), func=AF.Identity),
                                PK(bank), [('lav', g * TPC + i) for i in range(TPC)])
                    for hh in range(2):
                        slot = wload(win[WIN_IDX[('gg', 2 * p + hh)]])
                        for ch in range(NCH):
                            bank = gbank()
                            proj_fm(slot, big1, [('b1', ch * TPC + i) for i in range(TPC)], ch, bank)
                            ACT(lambda e, ch=ch, bank=bank, hh=hh: e.activation(
                                out=gateT[:, hh, ch * CW:(ch + 1) * CW], in_=ps[bank][:, 0:CW], func=AF.Silu),
                                PK(bank), [('gate', hh, ch)])
                    for g in range(NCH):
                        h = g % 2

                        def f(e, g=g, h=h):
                            for i in range(TPC):
                                c = g * TPC + i
                                ins = e.matmul(ps[TB[h]][:, i * 128:(i + 1) * 128], lhsT=kT[:, c * P:(c + 1) * P],
                                               rhs=identb[:], start=True, stop=True)
                            return ins
                        PE(f, [('kT', g)], PK(TB[h]))
                        ACT(lambda e, g=g, h=h: e.activation(
                            out=ktok[:, g * TPC:(g + 1) * TPC, :],
                            in_=ps[TB[h]][:, 0:CW].rearrange("p (a b) -> p a b", b=128), func=AF.Identity),
                            PK(TB[h]), [('ktok', g)])
                    for c in range(NT):
                        g = c // TPC
                        cs = slice(c * P, (c + 1) * P)
                        osl = c % TPC
                        for hh in range(2):
                            rr = slice(64 * hh, 64 * hh + 64)
                            sl = (c % 2) * 2 + hh
                            ab = 2 + hh
                            PE(lambda e, rr=rr, sl=sl, cs=cs, ab=ab: e.matmul(
                                ps[ab][:, sl * 128:(sl + 1) * 128], lhsT=kT[rr, cs], rhs=qT[rr, cs], start=True,
                                stop=True), [('kT', g), ('qT', g)], PK(ab))
                            DVE(lambda e, sl=sl, ab=ab: e.tensor_tensor(out=ATm[:, sl, :], in0=ps[ab][:, sl * 128:(sl + 1) * 128],
                                                                 in1=Ub[:], op=ALU.mult),
                                PK(ab), [('ATm', sl)])
                        for hh in range(2):
                            rr = slice(64 * hh, 64 * hh + 64)
                            sl = (c % 2) * 2 + hh
                            sb_ = hh
                            PE(lambda e, sl=sl, c=c, hh=hh, sb_=sb_: e.matmul(
                                ps[sb_][:, sl * 128:(sl + 1) * 128], lhsT=ktok[:, c, :],
                                rhs=vv[:, c, hh * 128:(hh + 1) * 128], start=True, stop=True),
                                [('ktok', g), ('lav', c)], PK(sb_))
                        for hh in range(2):
                            rr = slice(64 * hh, 64 * hh + 64)
                            sl = (c % 2) * 2 + hh
                            ob = 4 + hh

                            def f(e, sl=sl, c=c, hh=hh, rr=rr, cs=cs, ob=ob, osl=osl):
                                ins = e.matmul(ps[ob][:, osl * 128:(osl + 1) * 128], lhsT=vv[:, c, hh * 128:(hh + 1) * 128],
                                               rhs=ATm[:, sl, :], start=True, stop=(c == 0))
                                if c > 0:
                                    ins = e.matmul(ps[ob][:, osl * 128:(osl + 1) * 128], lhsT=Sbf[rr, (c - 1) % 2, :],
                                                   rhs=qT[rr, cs], start=False, stop=True)
                                return ins
                            PE(f, [('ATm', sl), ('lav', c), ('qT', g)] + ([('Sbf', hh, (c - 1) % 2)] if c > 0 else []),
                               PK(ob))
                        for hh in range(2):
                            rr = slice(64 * hh, 64 * hh + 64)
                            sl = (c % 2) * 2 + hh
                            sb_ = hh
                            if c == 0:
                                DVE(lambda e, rr=rr, sl=sl, sb_=sb_: e.tensor_copy(out=Tst[rr, :], in_=ps[sb_][rr, sl * 128:(sl + 1) * 128]),
                                    PK(sb_), [('Tst', hh)])
                            else:
                                DVE(lambda e, rr=rr, sl=sl, c=c, sb_=sb_: e.scalar_tensor_tensor(
                                    out=Tst[rr, :], in0=Tst[rr, :], scalar=elast[rr, c - 1:c],
                                    in1=ps[sb_][rr, sl * 128:(sl + 1) * 128], op0=ALU.mult, op1=ALU.add),
                                    PK(sb_) + [('Tst', hh), ('elast',)], [('Tst', hh)])
                            if c < NT - 1:
                                ACT(lambda e, rr=rr, c=c: e.activation(out=Sbf[rr, c % 2, :], in_=Tst[rr, :],
                                                                       func=AF.Identity, scale=elast[rr, c:c + 1]),
                                    [('Tst', hh), ('elast',)], [('Sbf', hh, c % 2)])
                        if c % TPC == TPC - 1:
                            for hh in range(2):
                                epilogue(4 + hh, None, gateT[:, hh, g * CW:(g + 1) * CW], [('gate', hh, g)],
                                         2 * p + hh, g * CW, CW, tmpG, hh)
                T.barrier()

            if stop == 'GLA':
                T.finish()
                return nc
            with ExitStack() as ar:
                fv = sb("fv", [P, NT, 512], BF16, ar)
                fqT = sb("fqT", [P, S], BF16, ar)
                fkT = sb("fkT", [P, S], BF16, ar)
                fgT = sb("fgT", [P, S], BF16, ar)
                lf = sb("lf", [P, NT * 4], F32, ar)
                ctok = sb("ctok", [P, NT * 4], F32, ar)
                cref = sb("cref", [P, NT * 4], F32, ar)
                biasT = sb("biasT", [P, NT, NT], F32, ar)
                PT = sb("PT", [P, 4, P], BF16, ar)
                ltmp = sb("ltmp", [P, 2, 64], F32, ar)
                tmpF = {'sq': [sb(f"sqf{i}", [P, 512], BF16, ar) for i in range(2)],
                        'rs': [sb(f"rsf{i}", [P, 512], F32, ar) for i in range(2)],
                        'rden': [sb(f"rdf{i}", [P, 512], F32, ar) for i in range(2)],
                        'on': [sb(f"onf{i}", [P, 512], F32, ar) for i in range(2)],
                        'lsa': ltmp[:, 0, :], 'lsl': ltmp[:, 1, :]}
                slot = wload(win[WIN_IDX[('small', 0)]])

                def f(e, slot=slot):
                    for t in range(NT):
                        for kc in range(KC):
                            e.matmul(ps[0][:, t * 4:(t + 1) * 4], lhsT=big1[:, kc, t * P:(t + 1) * P],
                                     rhs=ring[:, slot, kc * 128 + 32:kc * 128 + 36], start=(kc == 0), stop=False)
                        ins = e.matmul(ps[0][:, t * 4:(t + 1) * 4], lhsT=onesf[0:1, :], rhs=rowps[0:1, 512:516],
                                       start=False, stop=True)
                    return ins
                PE(f, [('ring', slot)] + [('b1', t) for t in range(NT)], PK(0))
                logsig(ps[0][:, 0:NT * 4], PK(0), lf[:], [('lf',)], NT * 4, tmpF)
                lf3 = lf[:].rearrange("p (b h) -> p b h", h=4)

                def f(e):
                    ins = e.matmul(ps[1][:, 0:NT * 4], lhsT=Uf, rhs=lf[:], start=True, stop=(NT == 1))
                    for b in range(NT - 1):
                        nb = NT - 1 - b
                        ins = e.matmul(ps[1][:, (b + 1) * 4:NT * 4].rearrange("p (b h) -> p b h", h=4), lhsT=onesf,
                                       rhs=lf3[:, b:b + 1, :].to_broadcast([P, nb, 4]),
                                       start=False, stop=(b == NT - 2))
                    return ins
                PE(f, [('lf',)], PK(1))
                ACT(lambda e: e.activation(out=ctok[:], in_=ps[1][:, 0:NT * 4], func=AF.Identity), PK(1), [('ctok',)])
                DVE(lambda e: e.memset(cref[:], 0.0), [], [('cref',)])
                if NT > 1:
                    def f(e):
                        for b in range(NT - 1):
                            nb = NT - 1 - b
                            ins = e.matmul(ps[0][:, (b + 1) * 4:NT * 4].rearrange("p (b h) -> p b h", h=4), lhsT=onesf,
                                           rhs=lf3[:, b:b + 1, :].to_broadcast([P, nb, 4]),
                                           start=(b == 0), stop=(b == NT - 2))
                        return ins
                    PE(f, [('lf',)], PK(0))
                    ACT(lambda e: e.activation(out=cref[:, 4:NT * 4], in_=ps[0][:, 4:NT * 4], func=AF.Identity),
                        PK(0) + [('cref',)], [('cref',)])
                for h in range(4):
                    slot = wload(win[WIN_IDX[('fv', h)]])
                    for g in range(NCH):
                        bank = gbank()

                        def f(e, g=g, bank=bank, slot=slot):
                            for i in range(TPC):
                                t = g * TPC + i
                                for kc in range(KC):
                                    ins = e.matmul(ps[bank][:, i * 128:(i + 1) * 128],
                                                   lhsT=big1[:, kc, t * P:(t + 1) * P],
                                                   rhs=ring[:, slot, kc * 128:(kc + 1) * 128],
                                                   start=(kc == 0), stop=(kc == KC - 1))
                            return ins
                        PE(f, [('ring', slot)] + [('b1', g * TPC + i) for i in range(TPC)], PK(bank))
                        ACT(lambda e, g=g, bank=bank, h=h: e.activation(
                            out=fv[:, g * TPC:(g + 1) * TPC, h * 128:(h + 1) * 128],
                            in_=ps[bank][:, 0:CW].rearrange("p (a b) -> p a b", b=128), func=AF.Identity),
                            PK(bank), [('fv', h)])
                ctok3 = ctok[:].rearrange("p (b h) -> p b h", h=4)
                for h in range(4):
                    for (nm, dst, kd, gi) in (('fq', fqT, 'fqT', 0), ('fk', fkT, 'fkT', 1)):
                        slot = wload(win[WIN_IDX[(nm, h)]])
                        for ch in range(NCH):
                            bank = gbank()
                            proj_fm(slot, big1, [('b1', ch * TPC + i) for i in range(TPC)], ch, bank)
                            qknorm(bank, pv[:, gi:gi + 1], dst[:, ch * CW:(ch + 1) * CW], [(kd, ch)], CW, tmpF, ch % 2)
                    slot = wload(win[WIN_IDX[('fg', h)]])
                    for ch in range(NCH):
                        bank = gbank()
                        proj_fm(slot, big1, [('b1', ch * TPC + i) for i in range(TPC)], ch, bank)
                        ACT(lambda e, ch=ch, bank=bank: e.activation(out=fgT[:, ch * CW:(ch + 1) * CW],
                                                                    in_=ps[bank][:, 0:CW], func=AF.Sigmoid),
                            PK(bank), [('fg', ch)])
                    for ib in range(NT):
                        DVE(lambda e, ib=ib, h=h: e.tensor_scalar(
                            out=biasT[:, ib, 0:ib + 1], in0=ctok3[:, 0:ib + 1, h], scalar1=-1.0,
                            scalar2=cref[:, ib * 4 + h:ib * 4 + h + 1], op0=ALU.mult, op1=ALU.add),
                            [('ctok',), ('cref',)], [('bias', ib)])
                    steps = [(ib, jb) for ib in range(NT) for jb in range(ib + 1)]
                    LA = 2

                    def emit_st(k):
                        ib, jb = steps[k]
                        sl = k % 4
                        stb = (2, 0, 7)[k % 3]
                        PE(lambda e, ib=ib, jb=jb, stb=stb: e.matmul(
                            ps[stb][:, 0:128], lhsT=fkT[:, jb * P:(jb + 1) * P],
                            rhs=fqT[:, ib * P:(ib + 1) * P], start=True, stop=True),
                            [('fkT', jb // TPC), ('fqT', ib // TPC)], PK(stb))
                        ACT(lambda e, ib=ib, jb=jb, sl=sl, stb=stb: e.activation(
                            out=PT[:, sl, :], in_=ps[stb][:, 0:128], func=AF.Exp,
                            bias=biasT[:, ib, jb:jb + 1], scale=128 ** -0.5),
                            PK(stb) + [('bias', ib)], [('PT', sl)])
                        if ib == jb:
                            DVE(lambda e, sl=sl: e.tensor_tensor(out=PT[:, sl, :], in0=PT[:, sl, :], in1=Ub[:],
                                                                 op=ALU.mult), [('PT', sl)], [('PT', sl)])
                    for k in range(min(LA, len(steps))):
                        emit_st(k)
                    for k, (ib, jb) in enumerate(steps):
                        if k + LA < len(steps):
                            emit_st(k + LA)
                        sl = k % 4
                        gg = ib // TPC
                        ob = 4 + gg % 2
                        db = (3, 1)[gg % 2]
                        osl = ib % TPC

                        def f(e, ib=ib, jb=jb, sl=sl, ob=ob, db=db, osl=osl, h=h):
                            e.matmul(ps[ob][:, osl * 128:(osl + 1) * 128], lhsT=fv[:, jb, h * 128:(h + 1) * 128],
                                     rhs=PT[:, sl, :], start=(jb == 0), stop=(jb == ib))
                            return e.matmul(ps[db][:, osl * 128:(osl + 1) * 128], lhsT=onesb[:], rhs=PT[:, sl, :],
                                            start=(jb == 0), stop=(jb == ib))
                        PE(f, [('PT', sl), ('fv', h)], PK(ob, osl, 1) + PK(db, osl, 1))
                        if jb == ib and ib % TPC == TPC - 1:
                            epilogue(ob, db, fgT[:, gg * CW:(gg + 1) * CW], [('fg', gg)], 8 + h, gg * CW, CW, tmpF,
                                     gg % 2)
                T.barrier()

            if stop == 'FOX':
                T.finish()
                return nc
            with ExitStack() as ar:
                mqT = sb("mqT", [P, S], BF16, ar)
                mgT = sb("mgT", [P, S], BF16, ar)
                PTm = sb("PTm", [P, 2, 2, 512], BF16, ar)
                tmpM = {'sq': [sb(f"sqm{i}", [P, 512], BF16, ar) for i in range(2)],
                        'rs': [sb(f"rsm{i}", [P, 512], F32, ar) for i in range(2)],
                        'rden': [sb(f"rdm{i}", [P, 512], F32, ar) for i in range(2)],
                        'on': [sb(f"onm{i}", [P, 512], F32, ar) for i in range(2)]}
                for h in range(4):
                    slot = wload(win[WIN_IDX[('mq', h)]])
                    for ch in range(NCH):
                        bank = gbank()
                        proj_fm(slot, big1, [('b1', ch * TPC + i) for i in range(TPC)], ch, bank)
                        qknorm(bank, pv[:, 2:3], mqT[:, ch * CW:(ch + 1) * CW], [('mqT', ch)], CW, tmpM, ch % 2)
                    slot = wload(win[WIN_IDX[('mg', h)]])
                    for ch in range(NCH):
                        bank = gbank()
                        proj_fm(slot, big1, [('b1', ch * TPC + i) for i in range(TPC)], ch, bank)
                        ACT(lambda e, ch=ch, bank=bank: e.activation(out=mgT[:, ch * CW:(ch + 1) * CW],
                                                                    in_=ps[bank][:, 0:CW], func=AF.Sigmoid),
                            PK(bank), [('mg', ch)])
                    for ch in range(NCH):
                        par = ch % 2
                        for mc in range(2):
                            PE(lambda e, mc=mc, ch=ch, h=h: e.matmul(
                                ps[2 + mc][:, 0:CW], lhsT=mkT[:, h, mc * P:(mc + 1) * P],
                                rhs=mqT[:, ch * CW:(ch + 1) * CW], start=True, stop=True),
                                [('mkT', h), ('mqT', ch)], PK(2 + mc))
                            ACT(lambda e, mc=mc, par=par: e.activation(out=PTm[:, par, mc, 0:CW], in_=ps[2 + mc][:, 0:CW],
                                                                       func=AF.Exp, scale=128 ** -0.5),
                                PK(2 + mc), [('PTm', par, mc)])
                        ob, db = 4 + par, (0, 1)[par]

                        def f(e, par=par, ob=ob, db=db, h=h):
                            for mc in range(2):
                                e.matmul(ps[ob][:, 0:CW], lhsT=mv[:, mc, h * 128:(h + 1) * 128], rhs=PTm[:, par, mc, 0:CW],
                                         start=(mc == 0), stop=(mc == 1))
                            for mc in range(2):
                                ins = e.matmul(ps[db][:, 0:CW], lhsT=onesb[:], rhs=PTm[:, par, mc, 0:CW],
                                               start=(mc == 0), stop=(mc == 1))
                            return ins
                        PE(f, [('PTm', par, 0), ('PTm', par, 1), ('mv', h)], PK(ob) + PK(db))
                        epilogue(ob, db, mgT[:, ch * CW:(ch + 1) * CW], [('mg', ch)], 12 + h, ch * CW, CW, tmpM, par)
                T.barrier()

            if stop == 'MEM':
                T.finish()
                return nc
            with ExitStack() as ar:
                hst = sb("hst", [P, TPC, D], F32, ar)
                xnb = sb("xnb2", [P, 2, D], BF16, ar)
                gbc = sb("gbc2", [P, D], F32, ar)
                sqj = sb("sqj2", [P, D], BF16, ar)
                load_gbc(2)
                for tt in range(NCH):
                    for sub in range(TPC):
                        t = tt * TPC + sub
                        T.dma('sp', ('hst', sub), lambda e, t=t, sub=sub: e.dma_start(
                            out=hst[:, sub, :], in_=x[r0 + t * P:r0 + (t + 1) * P, :]), writes=[('hst', sub)])
                    for j in range(16):
                        slot = wload(wout[j])
                        bank = gbank((0, 1, 2, 3))

                        def f(e, slot=slot, bank=bank, tt=tt):
                            for sub in range(TPC):
                                t = tt * TPC + sub
                                for hc in range(16):
                                    ins = e.matmul(ps[bank][:, sub * 128:(sub + 1) * 128],
                                                   lhsT=ogT[:, hc, t * P:(t + 1) * P],
                                                   rhs=ring[:, slot, hc * 128:(hc + 1) * 128],
                                                   start=(hc == 0), stop=(hc == 15))
                            return ins
                        PE(f, [('ring', slot)] + [('og', hc, tt) for hc in range(16)], PK(bank))
                        DVE(lambda e, bank=bank, j=j: e.tensor_tensor(
                            out=hst[:, :, j * 128:(j + 1) * 128], in0=ps[bank][:, 0:CW].rearrange("p (a b) -> p a b", b=128),
                            in1=hst[:, :, j * 128:(j + 1) * 128], op=ALU.add),
                            PK(bank) + [('hst', sub) for sub in range(TPC)], [('hst', sub) for sub in range(TPC)])
                    pend_s2 = None
                    for sub in range(TPC):
                        t = tt * TPC + sub
                        s2 = norm_rows(hst[:, sub, :], [('hst', sub)], gbc, xnb, sub % 2, big1, t * P, sub % 2)
                        T.dma('sp', ('hout', sub), lambda e, t=t, sub=sub: e.dma_start(
                            out=out[r0 + t * P:r0 + (t + 1) * P, :], in_=hst[:, sub, :]),
                            reads=[('hst', sub)], writes=[('outrow', s, t)], is_out=True)
                        if pend_s2 is not None:
                            pend_s2()
                        pend_s2 = s2
                    pend_s2()
                T.barrier()

            if stop == 'B1':
                T.finish()
                return nc
            with ExitStack() as ar:
                hres = sb("hres", [P, 2, TPC, 512], F32, ar)
                rtmp = sb("rtmp", [P, 2, 512], F32, ar)
                for tt in range(NCH):
                    c0 = tt * CW
                    for f_ in range(64):
                        slot = wload(wup[f_])
                        bank = gbank((4, 5, 6))
                        proj_fm(slot, big1, [('b1', tt * TPC + i) for i in range(TPC)], tt, bank)
                        par = f_ % 2
                        ACT(lambda e, bank=bank, par=par: e.activation(out=rtmp[:, par, 0:CW], in_=ps[bank][:, 0:CW],
                                                                       func=AF.Relu), PK(bank), [('rtmp', par)])
                        DVE(lambda e, par=par, f_=f_: e.tensor_tensor(out=uT[:, f_, 0:CW], in0=rtmp[:, par, 0:CW],
                                                                       in1=rtmp[:, par, 0:CW], op=ALU.mult),
                            [('rtmp', par)], [('uT', f_)])
                    for j in range(4):
                        hp = j % 2
                        for sub in range(TPC):
                            t = tt * TPC + sub
                            T.dma('sp', ('hres', hp, sub), lambda e, t=t, sub=sub, hp=hp, j=j: e.dma_start(
                                out=hres[:, hp, sub, :], in_=out[r0 + t * P:r0 + (t + 1) * P, j * 512:(j + 1) * 512]),
                                reads=[('outrow', s, t)], writes=[('hres', hp, sub)])
                        for fg in range(16):
                            slot = wload(wdn[j * 16 + fg])

                            for sub in range(TPC):
                                def f(e, slot=slot, fg=fg, sub=sub):
                                    for fi in range(4):
                                        ff = fg * 4 + fi
                                        ins = e.matmul(ps[sub][:, :], lhsT=uT[:, ff, sub * P:(sub + 1) * P],
                                                       rhs=ring[:, slot, fi * 512:(fi + 1) * 512],
                                                       start=(ff == 0), stop=(ff == 63))
                                    return ins
                                PE(f, [('ring', slot)] + [('uT', fg * 4 + fi) for fi in range(4)], PK(sub))
                        for sub in range(TPC):
                            t = tt * TPC + sub
                            DVE(lambda e, sub=sub, hp=hp: e.tensor_tensor(out=hres[:, hp, sub, :], in0=ps[sub][:, :],
                                                                         in1=hres[:, hp, sub, :], op=ALU.add),
                                PK(sub) + [('hres', hp, sub)], [('hres', hp, sub)])
                            T.dma('sp', ('yout', hp, sub), lambda e, t=t, sub=sub, hp=hp, j=j: e.dma_start(
                                out=out[r0 + t * P:r0 + (t + 1) * P, j * 512:(j + 1) * 512], in_=hres[:, hp, sub, :]),
                                reads=[('hres', hp, sub)], writes=[('outfin', s, t, j)], is_out=True)
                T.barrier()
        T.finish()
    return nc


def _host_layouts(inp):
    f32 = np.float32
    w_in = np.asarray(inp["w_in"], f32)
    blocks = []
    for (nm, i) in WIN_BLOCKS:
        if nm == 'small':
            blk = np.zeros((D, 128), f32)
            blk[:, 0:16] = w_in[:, 3072:3088]
            blk[:, 32:36] = w_in[:, 5136:5140]
        else:
            c = win_cols(nm, i)
            blk = w_in[:, c[0]:c[-1] + 1]
        blocks.append(blk.reshape(KC, P, 128).transpose(1, 0, 2).reshape(P, 2048))
    win = np.ascontiguousarray(np.stack(blocks))

    def colblocks(w, nb):
        K = w.shape[0] // P
        return np.ascontiguousarray(
            w.reshape(K, P, nb, 128).transpose(2, 1, 0, 3).reshape(nb, P, K * 128))

    wmem = colblocks(np.asarray(inp["w_mem_kv"], f32), 8)
    wout = colblocks(np.asarray(inp["w_out"], f32), 16)
    wup = colblocks(np.asarray(inp["w_up"], f32), 64)
    wd = np.asarray(inp["w_down"], f32)
    wdn = np.ascontiguousarray(
        wd.reshape(16, 4, P, 4, 512).transpose(3, 0, 2, 1, 4).reshape(64, P, 2048))
    consts = np.zeros((P, 384), f32)
    consts[:, 0:128] = np.eye(P, dtype=f32)
    consts[:, 128:256] = np.triu(np.ones((P, P), f32))
    consts[:, 256:384] = 1.0
    pvec = np.zeros((P, 20), f32)
    pvec[:, 0] = inp["fox_q_norm_g"]
    pvec[:, 1] = inp["fox_k_norm_g"]
    pvec[:, 2] = inp["mem_q_norm_g"]
    pvec[:, 3] = inp["mem_k_norm_g"]
    pvec[:, 4:20] = np.asarray(inp["out_norm_g"], f32).reshape(16, P).T
    gvecs = np.stack([inp["attn_norm_g"], inp["mem_norm_g"], inp["mlp_norm_g"]]).astype(f32)
    rowp = np.concatenate([inp["gla_a_b"], inp["fox_f_b"]]).astype(f32)[None, :]
    return dict(consts=consts, pvec=pvec, gvecs=np.ascontiguousarray(gvecs),
                w2=np.ascontiguousarray(inp["gla_a_w2"], dtype=f32), rowp=np.ascontiguousarray(rowp),
                win=win, wmem=wmem, wout=wout, wup=wup, wdn=wdn)


def run(inputs, n_cores, trace=False):
    x = np.asarray(inputs["x"], np.float32)
    mem = np.asarray(inputs["mem"], np.float32)
    B, S, _ = x.shape
    nseq = B // n_cores
    shared = _host_layouts(inputs)
    import os
    nc = build(nseq, S, os.environ.get('KSTOP'))
    in_maps = []
    for c in range(n_cores):
        m = dict(shared)
        m["x"] = np.ascontiguousarray(x[c * nseq:(c + 1) * nseq].reshape(nseq * S, D))
        m["mem"] = np.ascontiguousarray(mem[c * nseq:(c + 1) * nseq].reshape(nseq * 256, D))
        in_maps.append(m)
    res = run_bass_kernel_spmd(nc, in_maps, core_ids=list(range(n_cores)), trace=trace)
    outs = [r["out"].reshape(nseq, S, D) for r in res.results]
    return np.concatenate(outs, axis=0).astype(np.float32), res


def kernel(**inputs):
    y, _ = run(inputs, 8)
    return y
```
